# Optimizing a Trainium2 kernel written in Bass

```python
import math
import jax, jax.numpy as jnp
from jax import lax
import numpy as np

D_MODEL = 2048
BATCH = 8
SEQ = 4096
DEPTH = 2

CHUNK = 64
Q_BLOCK = 128
MIX_WIDTH = D_MODEL // 2
N_BRANCH = 3
H_A = MIX_WIDTH // 128
DH_A = 64
H_B = MIX_WIDTH // 128
DK_B = MIX_WIDTH // H_B
DV_B = MIX_WIDTH // H_B
H_C = 4
DK_C = MIX_WIDTH // (2 * H_C)
DV_C = MIX_WIDTH // H_C
GLA_GATE_RANK = 16
GLA_GATE_NORM = 16.0
D_FF = 11 * D_MODEL // 4
N_BUCKETS = 32
MAX_DISTANCE = 128
EPS = 1e-6

IN_SIZES = (
    H_A * 2 * DH_A, H_A * 2 * DH_A, H_A * 2 * DH_A,
    H_B * DK_B, H_B * DK_B, H_B * DV_B, H_B * DV_B,
    H_C * DK_C, H_C * DK_C, H_C * DV_C, H_C * DV_C, GLA_GATE_RANK,
    N_BRANCH * D_MODEL,
)
D_IN = sum(IN_SIZES)

kernel_name = 'chunk_causal_hybrid_diffattn_hgrn2_gla_macaron'


def rms_norm(x, gain):
    x32 = x.astype(jnp.float32)
    y = x32 * lax.rsqrt(jnp.mean(x32 * x32, axis=-1, keepdims=True) + EPS)
    return (y * gain.astype(jnp.float32)).astype(x.dtype)


def modulate(x, gain, shift, scale):
    return rms_norm(x, gain) * (1.0 + scale[:, None, :]) + shift[:, None, :]


def swiglu(h, w_gate, w_up, w_down):
    return (jax.nn.silu(h @ w_gate) * (h @ w_up)) @ w_down


def split_cols(u):
    out, idx = [], 0
    for size in IN_SIZES:
        out.append(u[..., idx:idx + size])
        idx += size
    return out


def t5_bucket(rel):
    half = N_BUCKETS // 2
    max_exact = half // 2
    ret = jnp.where(rel > 0, half, 0)
    n = jnp.abs(rel)
    nf = jnp.maximum(n, 1).astype(jnp.float32)
    large = max_exact + (jnp.log(nf / max_exact) / math.log(MAX_DISTANCE / max_exact)
                         * (half - max_exact)).astype(jnp.int32)
    large = jnp.minimum(large, half - 1)
    return ret + jnp.where(n < max_exact, n, large)


def diff_attention(q, k, v, rel_bias, lam, lam_init, out_gain):
    seq = q.shape[1]
    scale = DH_A ** -0.5
    outs = []
    for blk in range(seq // Q_BLOCK):
        q0, end = blk * Q_BLOCK, (blk + 1) * Q_BLOCK
        logits = jnp.einsum('bqhmd,bkhmd->bhmqk', q[:, q0:end], k[:, :end]).astype(jnp.float32) * scale
        qpos = jnp.arange(q0, end)
        kpos = jnp.arange(end)
        bias = rel_bias[t5_bucket(kpos[None, :] - qpos[:, None])].astype(jnp.float32)
        logits = logits + jnp.transpose(bias, (2, 0, 1))[None, :, None]
        visible = (kpos[None, :] // CHUNK) <= (qpos[:, None] // CHUNK)
        probs = jax.nn.softmax(jnp.where(visible, logits, -jnp.inf), axis=-1)
        weights = probs[:, :, 0] - lam * probs[:, :, 1]
        outs.append(jnp.einsum('bhqk,bkhe->bqhe', weights.astype(v.dtype), v[:, :end]))
    o = jnp.concatenate(outs, axis=1)
    return rms_norm(o, out_gain) * (1.0 - lam_init)


def chunk_gated_linear_attention(q, k, v, log_f):
    bsz, seq, heads, dk = q.shape
    dv = v.shape[-1]
    n = seq // CHUNK

    def to_chunks(t):
        return t.astype(jnp.float32).reshape(bsz, n, CHUNK, heads, t.shape[-1]).transpose(1, 0, 3, 2, 4)

    qc, kc, vc, gc = to_chunks(q), to_chunks(k), to_chunks(v), to_chunks(log_f)
    causal = jnp.tril(jnp.ones((CHUNK, CHUNK), dtype=bool))

    def step(state, inp):
        qi, ki, vi, gi = inp
        b = jnp.cumsum(gi, axis=-2)
        diff = b[..., :, None, :] - b[..., None, :, :]
        decay = jnp.exp(jnp.where(causal[:, :, None], diff, -jnp.inf))
        scores = jnp.einsum('bhtd,bhsd,bhtsd->bhts', qi, ki, decay)
        o = jnp.einsum('bhts,bhsv->bhtv', scores, vi) + jnp.einsum('bhtd,bhdv->bhtv', qi * jnp.exp(b), state)
        b_last = b[..., -1:, :]
        state = jnp.exp(b_last[..., 0, :])[..., :, None] * state + jnp.einsum(
            'bhsd,bhsv->bhdv', ki * jnp.exp(b_last - b), vi)
        return state, o

    state0 = jnp.zeros((bsz, heads, dk, dv), jnp.float32)
    _, o = lax.scan(step, state0, (qc, kc, vc, gc))
    return o.transpose(1, 0, 3, 2, 4).reshape(bsz, seq, heads, dv).astype(v.dtype)


def hybrid_mixer(h, lam_init, w_in, qk_gain, lam_vec, diff_gain, rel_bias, lb,
                 hgrn_gain, gla_w_up, gla_b, gla_gain, w_branch, w_out):
    bsz, seq = h.shape[0], h.shape[1]
    u = h @ w_in
    aq, ak, av, bq, bf, bi, bg, cq, ck, cv, cr, cgd, gate_logits = split_cols(u)

    aq = rms_norm(aq.reshape(bsz, seq, H_A, 2, DH_A), qk_gain[0])
    ak = rms_norm(ak.reshape(bsz, seq, H_A, 2, DH_A), qk_gain[1])
    lv = lam_vec.astype(jnp.float32)
    lam = jnp.exp(jnp.sum(lv[0] * lv[1])) - jnp.exp(jnp.sum(lv[2] * lv[3])) + lam_init
    ya = diff_attention(aq, ak, av.reshape(bsz, seq, H_A, 2 * DH_A), rel_bias, lam, lam_init,
                        diff_gain).reshape(bsz, seq, MIX_WIDTH)

    lb_h = lb.reshape(H_B, DK_B)
    zf = bf.reshape(bsz, seq, H_B, DK_B).astype(jnp.float32)
    log_f = jnp.logaddexp(jnp.log(lb_h), jnp.log1p(-lb_h) + jax.nn.log_sigmoid(zf))
    k_b = (1.0 - lb_h) * jax.nn.sigmoid(-zf)
    ob = chunk_gated_linear_attention(bq.reshape(bsz, seq, H_B, DK_B), k_b,
                                      bi.reshape(bsz, seq, H_B, DV_B), log_f)
    yb = rms_norm(ob * jax.nn.sigmoid(bg).reshape(bsz, seq, H_B, DV_B), hgrn_gain).reshape(bsz, seq, MIX_WIDTH)

    log_a = jax.nn.log_sigmoid((cgd @ gla_w_up + gla_b).astype(jnp.float32)) / GLA_GATE_NORM
    oc = chunk_gated_linear_attention(cq.reshape(bsz, seq, H_C, DK_C) * (DK_C ** -0.5),
                                      ck.reshape(bsz, seq, H_C, DK_C),
                                      cv.reshape(bsz, seq, H_C, DV_C),
                                      log_a.reshape(bsz, seq, H_C, DK_C))
    yc = (rms_norm(oc, gla_gain) * jax.nn.silu(cr.reshape(bsz, seq, H_C, DV_C))).reshape(bsz, seq, MIX_WIDTH)

    ys = jnp.stack([ya, yb, yc], axis=2)
    z = jnp.einsum('bsnc,ncd->bsnd', ys, w_branch)
    g = jax.nn.sigmoid(gate_logits.reshape(bsz, seq, N_BRANCH, D_MODEL))
    return jnp.sum(g * z, axis=2) @ w_out


def setup_inputs(seed: int = 0) -> dict:
    key = jax.random.key(seed)
    ks = jax.random.split(key, 20)

    def nrm(k, shape, scale):
        return jax.random.normal(k, shape, jnp.float32) * scale

    return {
        'x': nrm(ks[0], (BATCH, SEQ, D_MODEL), 1.0),
        'c': nrm(ks[1], (BATCH, D_MODEL), 1.0),
        'w_ada': nrm(ks[2], (DEPTH, D_MODEL, 9 * D_MODEL), 0.5 * D_MODEL ** -0.5),
        'b_ada': nrm(ks[3], (DEPTH, 9 * D_MODEL), 0.02),
        'norm_gains': 1.0 + nrm(ks[4], (DEPTH, 4, D_MODEL), 0.02),
        'ffn_w_gate': nrm(ks[5], (DEPTH, 2, D_MODEL, D_FF), D_MODEL ** -0.5),
        'ffn_w_up': nrm(ks[6], (DEPTH, 2, D_MODEL, D_FF), D_MODEL ** -0.5),
        'ffn_w_down': nrm(ks[7], (DEPTH, 2, D_FF, D_MODEL), D_FF ** -0.5),
        'w_in': nrm(ks[8], (DEPTH, D_MODEL, D_IN), D_MODEL ** -0.5),
        'qk_gains': 1.0 + nrm(ks[9], (DEPTH, 2, DH_A), 0.02),
        'diff_lambda': nrm(ks[10], (DEPTH, 4, DH_A), 0.1),
        'diff_out_gain': 1.0 + nrm(ks[11], (DEPTH, 2 * DH_A), 0.02),
        'rel_bias': nrm(ks[12], (N_BUCKETS, H_A), 0.5),
        'hgrn_lb_logits': nrm(ks[13], (DEPTH, H_B * DK_B), 1.0),
        'hgrn_out_gain': 1.0 + nrm(ks[14], (DEPTH, DV_B), 0.02),
        'gla_w_gate_up': nrm(ks[15], (DEPTH, GLA_GATE_RANK, H_C * DK_C), GLA_GATE_RANK ** -0.5),
        'gla_b_gate': nrm(ks[16], (DEPTH, H_C * DK_C), 0.1),
        'gla_out_gain': 1.0 + nrm(ks[17], (DEPTH, DV_C), 0.02),
        'w_branch': nrm(ks[18], (DEPTH, N_BRANCH, MIX_WIDTH, D_MODEL), MIX_WIDTH ** -0.5),
        'w_out': nrm(ks[19], (DEPTH, D_MODEL, D_MODEL), D_MODEL ** -0.5),
    }


def reference(x, c, w_ada, b_ada, norm_gains, ffn_w_gate, ffn_w_up, ffn_w_down, w_in,
              qk_gains, diff_lambda, diff_out_gain, rel_bias, hgrn_lb_logits, hgrn_out_gain,
              gla_w_gate_up, gla_b_gate, gla_out_gain, w_branch, w_out):
    lb_all = jnp.cumsum(jax.nn.softmax(hgrn_lb_logits.astype(jnp.float32), axis=0), axis=0)
    lb_all = lb_all - lb_all[0]
    cond = jax.nn.silu(c)
    bsz = c.shape[0]
    for l in range(DEPTH):
        mod = (cond @ w_ada[l] + b_ada[l]).reshape(bsz, 3, 3, D_MODEL)
        shift, scale, gate = mod[:, :, 0], mod[:, :, 1], mod[:, :, 2]
        lam_init = 0.8 - 0.6 * math.exp(-0.3 * l)

        h = modulate(x, norm_gains[l, 0], shift[:, 0], scale[:, 0])
        x = x + 0.5 * gate[:, 0, None, :] * swiglu(h, ffn_w_gate[l, 0], ffn_w_up[l, 0], ffn_w_down[l, 0])

        h = modulate(x, norm_gains[l, 1], shift[:, 1], scale[:, 1])
        x = x + gate[:, 1, None, :] * hybrid_mixer(
            h, lam_init, w_in[l], qk_gains[l], diff_lambda[l], diff_out_gain[l], rel_bias,
            lb_all[l], hgrn_out_gain[l], gla_w_gate_up[l], gla_b_gate[l], gla_out_gain[l],
            w_branch[l], w_out[l])

        h = modulate(x, norm_gains[l, 2], shift[:, 2], scale[:, 2])
        x = x + 0.5 * gate[:, 2, None, :] * swiglu(h, ffn_w_gate[l, 1], ffn_w_up[l, 1], ffn_w_down[l, 1])

        x = rms_norm(x, norm_gains[l, 3])
    return x
```

```python
import contextlib
import math
import numpy as np
import concourse.bass as bass
import concourse.mybir as mybir
from concourse.bass_utils import run_bass_kernel_spmd

F32 = mybir.dt.float32
BF16 = mybir.dt.bfloat16
AF = mybir.ActivationFunctionType
ALU = mybir.AluOpType
AX = mybir.AxisListType

ENGS = ['pe', 'act', 'dve', 'pool', 'sp']
COMPUTE = ['pe', 'act', 'dve', 'pool']

D = 2048
KC = 16
DFF = 5632
FC = 44
TT = 512
DIN = 16400
MIX = 1024
EPS = 1e-6
DEPTH = 2

O_AQ, O_AK, O_AV = 0, 1024, 2048
O_BQ, O_BF, O_BI, O_BG = 3072, 4096, 5120, 6144
O_CQ, O_CK, O_CV, O_CR, O_CGD, O_GATE = 7168, 7680, 8192, 9216, 10240, 10256


class Buf:
    __slots__ = ('name', 'w', 'r', 'dsem')

    def __init__(self, name):
        self.name = name
        self.w = None
        self.r = {}
        self.dsem = None


class DSem:
    __slots__ = ('sem', 'count', 'key')

    def __init__(self, sem, key):
        self.sem = sem
        self.count = 0
        self.key = key


class Rot:
    def __init__(self, items):
        self.items = items
        self.i = 0

    def next(self):
        it = self.items[self.i % len(self.items)]
        self.i += 1
        return it


class KB:
    def __init__(self):
        self.nc = bass.Bass("TRN2", target_bir_lowering=False)
        self.gstack = contextlib.ExitStack()
        self.stacks = [self.gstack]
        self.ops = {e: [] for e in ENGS}
        self.cnt = {e: 0 for e in ENGS}
        self.waited = {e: {} for e in ENGS}
        self.prog = {}
        self.dsems = []
        self.free_dsems = []
        self.dbufs = []
        self.nwaits = 0
        self.uid = 0
        self.strict = True
        for e in COMPUTE:
            self.prog[e] = self.gstack.enter_context(self.nc.semaphore('pg_' + e))

    def sb(self, name, shape, dt):
        self.uid += 1
        return self.stacks[-1].enter_context(
            self.nc.sbuf_tensor("%s_%d" % (name, self.uid), list(shape), dt))

    def sbb(self, name, shape, dt):
        return self.sb(name, shape, dt), Buf(name)

    def rot(self, name, n, shape, dt):
        return Rot([self.sbb("%s%d" % (name, i), shape, dt) for i in range(n)])

    def dram(self, name, shape, dt, kind="Internal"):
        return self.nc.dram_tensor(name, list(shape), dt, kind=kind).ap()

    def _dsem_for(self, b):
        if b.dsem is None:
            if self.free_dsems:
                b.dsem = self.free_dsems.pop()
            else:
                key = 'd%d' % len(self.dsems)
                sem = self.gstack.enter_context(self.nc.semaphore('ds%d' % len(self.dsems)))
                b.dsem = DSem(sem, key)
                self.dsems.append(b.dsem)
            self.dbufs.append(b)
        return b.dsem

    @contextlib.contextmanager
    def phase(self):
        st = contextlib.ExitStack()
        self.stacks.append(st)
        try:
            yield
        finally:
            self.barrier()
            self.stacks.pop()
            st.close()

    def _collect(self, eng, reads, writes, extra, is_dma=False):
        deps = {}

        def add(t):
            if t is None:
                return
            key, sem, val, peng = t
            if peng == eng and not is_dma and eng != 'pool' and not (self.strict and eng != 'pe'):
                return
            if key not in deps or deps[key][2] < val:
                deps[key] = t
        for b in reads:
            add(b.w)
        for b in writes:
            add(b.w)
            for t in b.r.values():
                add(t)
        for t in extra:
            add(t)
        out = []
        wd = self.waited[eng]
        for key, t in deps.items():
            if wd.get(key, 0) >= t[2]:
                continue
            wd[key] = t[2]
            out.append((t[1], t[2]))
        self.nwaits += len(out)
        return out

    def _update(self, tok, reads, writes):
        for b in writes:
            b.w = tok
            b.r = {}
        k = tok[0]
        for b in reads:
            if k not in b.r or b.r[k][2] < tok[2]:
                b.r[k] = tok

    def op(self, eng, fn, reads=(), writes=(), extra=()):
        waits = self._collect(eng, reads, writes, extra)
        self.cnt[eng] += 1
        tok = (eng, self.prog[eng], self.cnt[eng], eng)
        self.ops[eng].append((fn, waits, (self.prog[eng], 1)))
        self._update(tok, reads, writes)
        return tok

    def dma(self, q, out, in_, reads=(), writes=(), sembuf=None):
        waits = self._collect(q, reads, writes, (), is_dma=True)
        ds = self._dsem_for(sembuf if sembuf is not None else writes[0])
        ds.count += 16
        tok = (ds.key, ds.sem, ds.count, 'dma')

        def fn(e, out=out, in_=in_):
            return e.dma_start(out=out, in_=in_)
        self.ops[q].append((fn, waits, (ds.sem, 16)))
        self._update(tok, reads, writes)
        return tok

    def load(self, q, dst, dbuf, src):
        return self.dma(q, dst, src, reads=(), writes=[dbuf])

    def store(self, q, dst, src, sbuf_):
        return self.dma(q, dst, src, reads=[sbuf_], writes=(), sembuf=sbuf_)

    def barrier(self):
        toks = [(ds.sem, ds.count, ds.key) for ds in self.dsems if ds.count > 0]
        toks += [(self.prog[e], self.cnt[e], e) for e in COMPUTE if self.cnt[e] > 0]
        for e in ENGS:
            waits = []
            wd = self.waited[e]
            for sem, val, key in toks:
                if wd.get(key, 0) >= val:
                    continue
                wd[key] = val
                waits.append((sem, val))
            if waits:
                self.ops[e].append((None, waits, None))
        for b in self.dbufs:
            b.dsem = None
        self.dbufs = []
        self.free_dsems = list(self.dsems)

    def emit(self):
        self.barrier()
        nc = self.nc
        ops = self.ops

        def replay(name, e):
            for fn, waits, inc in ops[name]:
                for sem, val in waits:
                    e.wait_ge(sem, val)
                if fn is None:
                    continue
                ins = fn(e)
                if inc is not None:
                    ins.then_inc(inc[0], inc[1])

        with nc.Block() as block:
            @block.sync
            def _(e):
                replay('sp', e)

            @block.scalar
            def _(e):
                replay('act', e)

            @block.vector
            def _(e):
                replay('dve', e)

            @block.gpsimd
            def _(e):
                replay('pool', e)

            @block.tensor
            def _(e):
                replay('pe', e)
        self.gstack.close()
        return nc


def _t5_bucket_table():
    import jax
    import jax.numpy as jnp

    def t5_bucket(rel):
        half = 16
        max_exact = 8
        ret = jnp.where(rel > 0, half, 0)
        n = jnp.abs(rel)
        nf = jnp.maximum(n, 1).astype(jnp.float32)
        large = max_exact + (jnp.log(nf / max_exact) / math.log(128 / max_exact)
                             * (half - max_exact)).astype(jnp.int32)
        large = jnp.minimum(large, half - 1)
        return ret + jnp.where(n < max_exact, n, large)
    rel = np.arange(-255, 128, dtype=np.int32)
    try:
        cpu = jax.devices('cpu')[0]
        with jax.default_device(cpu):
            out = np.asarray(jax.jit(t5_bucket)(jnp.asarray(rel)))
    except Exception:
        n = np.abs(rel)
        nf = np.maximum(n, 1).astype(np.float32)
        large = 8 + (np.log(nf / np.float32(8)) / np.float32(math.log(16.0)) * np.float32(8)).astype(np.int32)
        large = np.minimum(large, 15)
        out = np.where(rel > 0, 16, 0) + np.where(n < 8, n, large)
    return {int(r): int(b) for r, b in zip(rel, out)}


def host_consts():
    bt = _t5_bucket_table()
    kk = np.arange(128)[:, None]
    qq = np.arange(128)[None, :]
    oht = np.zeros((2, 128, 33, 128), np.float32)
    for typ, off in ((0, 0), (1, -128)):
        rel = kk - qq + off
        bk = np.vectorize(lambda r: bt[int(r)])(rel)
        for b in range(32):
            oht[typ, :, b, :] = (bk == b)
    oht[0, :, 32, :] = np.where((kk // 64) <= (qq // 64), 0.0, -30000.0)
    cst = np.zeros((128, 128 + 128 + 512 + 512), np.float32)
    cst[:, 0:128] = np.eye(128)
    bd = np.zeros((128, 128), np.float32)
    bd[:64, :64] = 1.0
    bd[64:, 64:] = 1.0
    cst[:, 128:256] = bd
    rm = np.ones((128, 512), np.float32)
    rm[:, 0::32] = 0.0
    cst[:, 256:768] = rm
    s = np.arange(32)[:, None]
    t = np.arange(32)[None, :]
    cm = (s <= t).astype(np.float32)
    cst[:32, 768:1280] = np.tile(cm, (1, 16))
    return oht.reshape(2, 128, 33 * 128), cst


SC = {}


def _sc_layout():
    off = 0

    def add(name, n):
        nonlocal off
        SC[name] = (off, n)
        off += n
    add('c', 16)
    for l in range(DEPTH):
        add('bada%d' % l, 144)
        for i in range(4):
            add('ng%d_%d' % (l, i), 16)
        add('qg%d' % l, 1)
        add('kg%d' % l, 1)
        add('dog%d' % l, 1)
        add('hog%d' % l, 1)
        add('glab%d' % l, 4)
        add('glag%d' % l, 2)
        add('dl%d' % l, 256)
    add('lbl', 16)
    add('relb', 256)
    return off


NSC = _sc_layout()


def host_smallcols(b, inp):
    sc = np.zeros((128, NSC), np.float32)

    def put(name, arr):
        o, n = SC[name]
        sc[:, o:o + n] = arr
    put('c', inp['c'][b].reshape(16, 128).T)
    for l in range(DEPTH):
        put('bada%d' % l, inp['b_ada'][l].reshape(144, 128).T)
        for i in range(4):
            put('ng%d_%d' % (l, i), inp['norm_gains'][l, i].reshape(16, 128).T)
        put('qg%d' % l, np.tile(inp['qk_gains'][l, 0], 2)[:, None])
        put('kg%d' % l, np.tile(inp['qk_gains'][l, 1], 2)[:, None])
        put('dog%d' % l, inp['diff_out_gain'][l][:, None])
        put('hog%d' % l, inp['hgrn_out_gain'][l][:, None])
        put('glab%d' % l, inp['gla_b_gate'][l].reshape(4, 128).T)
        put('glag%d' % l, inp['gla_out_gain'][l].reshape(2, 128).T)
        put('dl%d' % l, np.broadcast_to(inp['diff_lambda'][l].reshape(1, 256), (128, 256)))
    put('lbl', inp['hgrn_lb_logits'].reshape(2, 8, 128).transpose(2, 0, 1).reshape(128, 16))
    put('relb', np.broadcast_to(inp['rel_bias'].reshape(1, 256), (128, 256)))
    return sc


class Prog:
    def __init__(self, T, debug=False, stop_after=None):
        self.T = T
        self.NT = T // TT
        self.debug = debug
        self.stop_after = stop_after
        self.k = KB()
        self.nc = self.k.nc

    def dbg(self, name, ap, buf, shape, dt):
        if not self.debug:
            return
        self.k.uid += 1
        if name in getattr(self, '_dbgseen', set()):
            name = "%s_%d" % (name, self.k.uid)
        self._dbgseen = getattr(self, '_dbgseen', set()) | {name}
        d = self.k.dram("dbg_" + name, list(shape), dt, kind="ExternalOutput")
        self.k.store('sp', d, ap, buf)

    def col(self, name, j=0, n=1):
        o, _ = SC[name]
        return self.sc[:, o + j:o + j + n]

    def build(self):
        k, nc, T = self.k, self.nc, self.T
        ext = "ExternalInput"
        self.xT = nc.dram_tensor("xT", [D, T], F32, kind=ext).ap()
        self.scd = nc.dram_tensor("smallcols", [128, NSC], F32, kind=ext).ap()
        self.cstd = nc.dram_tensor("cst", [128, 1280], F32, kind=ext).ap()
        self.ohtd = nc.dram_tensor("oht", [2, 128, 33 * 128], F32, kind=ext).ap()
        self.w_ada = nc.dram_tensor("w_ada", [DEPTH, D, 9 * D], F32, kind=ext).ap()
        self.wg = nc.dram_tensor("ffn_w_gate", [DEPTH, 2, D, DFF], F32, kind=ext).ap()
        self.wu = nc.dram_tensor("ffn_w_up", [DEPTH, 2, D, DFF], F32, kind=ext).ap()
        self.wd = nc.dram_tensor("ffn_w_down", [DEPTH, 2, DFF, D], F32, kind=ext).ap()
        self.w_in = nc.dram_tensor("w_in", [DEPTH, D, DIN], F32, kind=ext).ap()
        self.gwu = nc.dram_tensor("gla_w_gate_up", [DEPTH, 16, 512], F32, kind=ext).ap()
        self.w_br = nc.dram_tensor("w_branch", [DEPTH, 3, MIX, D], F32, kind=ext).ap()
        self.w_out = nc.dram_tensor("w_out", [DEPTH, D, D], F32, kind=ext).ap()
        self.outT = nc.dram_tensor("outT", [D, T], F32, kind="ExternalOutput").ap()
        sk = "ExternalOutput" if self.debug else "Internal"
        self.x = k.dram("s_x", [D, T], F32, kind=sk)
        self.h = k.dram("s_h", [D, T], BF16, kind=sk)
        self.aq = k.dram("s_aq", [MIX, T], BF16, kind=sk)
        self.ak = k.dram("s_ak", [MIX, T], BF16, kind=sk)
        self.vtok = [k.dram("s_v%d" % i, [T, MIX], BF16, kind=sk) for i in range(3)]
        self.bq = k.dram("s_bq", [MIX, T], BF16, kind=sk)
        self.bz = k.dram("s_bz", [MIX, T], F32, kind=sk)
        self.bsg = k.dram("s_bsg", [MIX, T], BF16, kind=sk)
        self.cq = k.dram("s_cq", [512, T], BF16, kind=sk)
        self.ck = k.dram("s_ck", [512, T], BF16, kind=sk)
        self.crs = k.dram("s_crs", [MIX, T], BF16, kind=sk)
        self.cgd = k.dram("s_cgd", [16, T], BF16, kind=sk)
        self.gat = k.dram("s_gat", [3 * D, T], BF16, kind=sk)
        self.y = [k.dram("s_y%d" % i, [MIX, T], BF16, kind=sk) for i in range(3)]
        self.dbg_mod = k.dram("s_mod", [128, 2 * 144], F32, kind=sk)

        self.pb = []
        for i in range(8):
            t = k.gstack.enter_context(nc.psum_tensor("pb%d" % i, [128, 512], F32))
            self.pb.append((t, Buf("pb%d" % i)))

        self.setup()
        stages = []
        for l in range(DEPTH):
            stages += [('norm', l, 0), ('ffn', l, 0), ('norm', l, 1), ('mixin', l), ('attn', l),
                       ('gla', l, 'B'), ('gla', l, 'C'), ('merge', l), ('norm', l, 2), ('ffn', l, 1),
                       ('fnorm', l)]
        for st in stages:
            getattr(self, 'ph_' + st[0])(*st[1:])
            if self.stop_after is not None and tuple(st) == tuple(self.stop_after):
                break
        return k.emit()

    def setup(self):
        k, nc = self.k, self.nc
        self.sc, self.sc_b = k.sbb("sc", [128, NSC], F32)
        self.cst, self.cst_b = k.sbb("cst", [128, 1280], F32)
        self.ones32, self.ones32_b = k.sbb("ones32", [128, 128], F32)
        self.ones16, self.ones16_b = k.sbb("ones16", [128, 128], BF16)
        self.ident16, self.ident16_b = k.sbb("ident16", [128, 128], BF16)
        self.biasT, self.biasT_b = k.sbb("biasT", [128, 8, 2, 128], F32)
        self.mod, self.mod_b = k.sbb("mod", [128, 2 * 144], F32)
        self.colsA, self.colsA_b = k.sbb("colsA", [128, 2 * 3 * 16], F32)
        self.colsG, self.colsG_b = k.sbb("colsG", [128, 2 * 3 * 16], F32)
        self.misc, self.misc_b = k.sbb("misc", [128, 64], F32)
        self.wup16, self.wup16_b = k.sbb("wup16", [16, 2 * 512], BF16)
        self.x_cur = self.xT

        with k.phase():
            k.load('sp', self.sc[:, :], self.sc_b, self.scd)
            k.load('sp', self.cst[:, :], self.cst_b, self.cstd)
            k.op('dve', lambda e: e.memset(self.ones32[:, :], 1.0), writes=[self.ones32_b])
            k.op('dve', lambda e: e.memset(self.ones16[:, :], 1.0), writes=[self.ones16_b])
            k.op('dve', lambda e: e.tensor_copy(self.ident16[:, :], self.cst[:, 0:128]),
                 reads=[self.cst_b], writes=[self.ident16_b])
            for l in range(DEPTH):
                k.dma('pool', self.wup16[:, l * 512:(l + 1) * 512], self.gwu[l], writes=[self.wup16_b])
            oh, oh_b = k.sbb("oh", [128, 33 * 128], F32)
            ro, _ = SC['relb']
            for typ in range(2):
                k.load('sp', oh[:, :], oh_b, self.ohtd[typ])
                for h in range(8):
                    dst = self.biasT[:, h, typ, :]
                    k.op('dve', lambda e, dst=dst: e.tensor_copy(dst, oh[:, 32 * 128:33 * 128]),
                         reads=[oh_b], writes=[self.biasT_b])
                    for b in range(32):
                        sc_ap = self.sc[:, ro + b * 8 + h:ro + b * 8 + h + 1]
                        k.op('dve', lambda e, dst=dst, b=b, sc_ap=sc_ap: e.scalar_tensor_tensor(
                            out=dst, in0=oh[:, b * 128:(b + 1) * 128], scalar=sc_ap, in1=dst,
                            op0=ALU.mult, op1=ALU.add),
                            reads=[oh_b, self.sc_b], writes=[self.biasT_b])
            k.strict = True
            tmp, tmp_b = k.sbb("stmp", [128, 64], F32)
            red, red_b = k.sbb("sred", [128, 4], F32)
            junk, junk_b = k.sbb("sjunk", [128, 64], F32)
            red2, red2_b = k.sbb("sred2", [128, 2], F32)
            for l in range(DEPTH):
                lam_init = 0.8 - 0.6 * math.exp(-0.3 * l)
                o, _ = SC['dl%d' % l]
                for pair in range(2):
                    a0 = self.sc[:, o + pair * 128:o + pair * 128 + 64]
                    a1 = self.sc[:, o + pair * 128 + 64:o + pair * 128 + 128]
                    k.op('dve', lambda e, a0=a0, a1=a1: e.tensor_tensor(tmp[:, :], a0, a1, ALU.mult),
                         reads=[self.sc_b], writes=[tmp_b])
                    k.op('dve', lambda e, pair=pair: e.tensor_tensor_scan(
                        out=junk[:, :], data0=self.ones32[:, 0:64], data1=tmp[:, :], initial=0.0,
                        op0=ALU.mult, op1=ALU.add), reads=[tmp_b, self.ones32_b], writes=[junk_b])
                    k.op('dve', lambda e, pair=pair: e.tensor_copy(red[:, pair:pair + 1], junk[:, 63:64]),
                         reads=[junk_b], writes=[red_b])
                for pair in range(2):
                    k.op('act', lambda e, pair=pair: e.activation(out=red2[:, pair:pair + 1], in_=red[:, pair:pair + 1],
                                                                   func=AF.Exp), reads=[red_b], writes=[red2_b])
                k.op('dve', lambda e, l=l, lam_init=lam_init: e.tensor_tensor(
                    self.misc[:, l * 16:l * 16 + 1], red2[:, 1:2], red2[:, 0:1], ALU.subtract),
                    reads=[red2_b], writes=[self.misc_b])
                k.op('dve', lambda e, l=l, lam_init=lam_init: e.tensor_scalar(
                    self.misc[:, l * 16:l * 16 + 1], self.misc[:, l * 16:l * 16 + 1], -lam_init, None,
                    op0=ALU.add), reads=[self.misc_b], writes=[self.misc_b])
                k.op('dve', lambda e, l=l, lam_init=lam_init: e.tensor_scalar(
                    self.misc[:, l * 16 + 1:l * 16 + 2], self.col('dog%d' % l), 1.0 - lam_init, None,
                    op0=ALU.mult), reads=[self.sc_b], writes=[self.misc_b])
            lo, _ = SC['lbl']
            k.op('dve', lambda e: e.tensor_tensor(tmp[:, 0:8], self.sc[:, lo + 8:lo + 16],
                                                  self.sc[:, lo:lo + 8], ALU.subtract),
                 reads=[self.sc_b], writes=[tmp_b])
            k.op('act', lambda e: e.activation(out=self.misc[:, 32:40], in_=tmp[:, 0:8], func=AF.Sigmoid),
                 reads=[tmp_b], writes=[self.misc_b])
            k.op('dve', lambda e: e.tensor_scalar(self.misc[:, 40:48], self.misc[:, 32:40], -1.0, 1.0,
                                                  op0=ALU.mult, op1=ALU.add),
                 reads=[self.misc_b], writes=[self.misc_b])
            k.op('dve', lambda e: e.memset(self.misc[:, 48:56], 0.0), writes=[self.misc_b])
            k.op('dve', lambda e: e.memset(self.misc[:, 56:64], 1.0), writes=[self.misc_b])

        with k.phase():
            cond2, cond2_b = k.sbb("cond2", [128, 16, 2], F32)
            co, _ = SC['c']
            for j in range(2):
                k.op('act', lambda e, j=j: e.activation(out=cond2[:, :, j], in_=self.sc[:, co:co + 16],
                                                         func=AF.Silu),
                     reads=[self.sc_b], writes=[cond2_b])
            wrot = k.rot("wada", 2, [128, 16, 512], F32)
            for l in range(DEPTH):
                ps, ps_b = self.pb[l]
                wv = self.w_ada[l].rearrange("(kc p) n -> p kc n", p=128)
                for s in range(36):
                    wt, wt_b = wrot.next()
                    k.load('sp', wt[:, :, :], wt_b, wv[:, :, s * 512:(s + 1) * 512])
                    for n in range(4):
                        cidx = s * 4 + n

                        def grp(e, wt=wt, n=n, cidx=cidx, ps=ps):
                            ins = None
                            for kc in range(KC):
                                ins = e.matmul(ps[:, 2 * cidx:2 * cidx + 2],
                                               lhsT=wt[:, kc, n * 128:(n + 1) * 128],
                                               rhs=cond2[:, kc, :], start=(kc == 0), stop=(kc == KC - 1))
                            return ins
                        k.op('pe', grp, reads=[wt_b, cond2_b], writes=[ps_b])
                bo, _ = SC['bada%d' % l]
                psv = ps[:, 0:288].rearrange("p (c t) -> p c t", t=2)[:, :, 0]
                k.op('dve', lambda e, l=l, psv=psv, bo=bo: e.tensor_tensor(
                    self.mod[:, l * 144:(l + 1) * 144], psv, self.sc[:, bo:bo + 144], ALU.add),
                    reads=[ps_b, self.sc_b], writes=[self.mod_b])
                for i in range(3):
                    base = l * 144 + i * 48
                    go, _ = SC['ng%d_%d' % (l, i)]
                    ao = (l * 3 + i) * 16
                    k.op('dve', lambda e, base=base, go=go, ao=ao: e.scalar_tensor_tensor(
                        out=self.colsA[:, ao:ao + 16], in0=self.mod[:, base + 16:base + 32], scalar=1.0,
                        in1=self.sc[:, go:go + 16], op0=ALU.add, op1=ALU.mult),
                        reads=[self.mod_b, self.sc_b], writes=[self.colsA_b])
                    gs = 1.0 if i == 1 else 0.5
                    k.op('dve', lambda e, base=base, ao=ao, gs=gs: e.tensor_scalar(
                        self.colsG[:, ao:ao + 16], self.mod[:, base + 32:base + 48], gs, None, op0=ALU.mult),
                        reads=[self.mod_b], writes=[self.colsG_b])
            if self.debug:
                k.store('sp', self.dbg_mod, self.mod[:, :], self.mod_b)
                self.dbg("misc", self.misc[:, :], self.misc_b, [128, 64], F32)
                self.dbg("sc", self.sc[:, :], self.sc_b, [128, NSC], F32)
                self.dbg("biasT", self.biasT[:, :, :, :], self.biasT_b, [128, 8, 2, 128], F32)

    def _norm(self, src, dst, out_dt, a_ap, b_ap):
        k = self.k
        srcv = src.rearrange("(kc p) t -> p kc t", p=128)
        dstv = dst.rearrange("(kc p) t -> p kc t", p=128)
        with k.phase():
            xrot = k.rot("nx", 3, [128, KC, TT], F32)
            orot = k.rot("no", 2, [128, KC, TT], out_dt)
            sqrot = k.rot("nsq", 3, [128, TT], F32)
            trot = k.rot("ntmp", 3, [128, TT], F32)
            stdrot = k.rot("nstd", 2, [128, TT], F32)
            rstdrot = k.rot("nrstd", 2, [128, TT], F32)

            def stats_a(tt):
                ts = slice(tt * TT, (tt + 1) * TT)
                xt, xt_b = xrot.next()
                k.load('sp', xt[:, :, :], xt_b, srcv[:, :, ts])
                ps, ps_b = self.pb[tt % 2]
                for kc in range(KC):
                    sq, sq_b = sqrot.next()
                    k.op('act', lambda e, sq=sq, kc=kc: e.activation(out=sq[:, :], in_=xt[:, kc, :], func=AF.Square),
                         reads=[xt_b], writes=[sq_b])
                    k.op('pe', lambda e, sq=sq, kc=kc: e.matmul(ps[:, :], lhsT=self.ones32[:, :], rhs=sq[:, :],
                                                                start=(kc == 0), stop=(kc == KC - 1)),
                         reads=[sq_b, self.ones32_b], writes=[ps_b])
                return dict(ts=ts, xt=xt, xt_b=xt_b, ps=ps, ps_b=ps_b)

            def stats_b(cx):
                ps, ps_b = cx['ps'], cx['ps_b']
                std, std_b = stdrot.next()
                rstd, rstd_b = rstdrot.next()
                k.op('act', lambda e: e.activation(out=std[:, :], in_=ps[:, :], func=AF.Ln,
                                                   bias=self.epsc[:, 0:1], scale=1.0 / D),
                     reads=[ps_b], writes=[std_b])
                k.op('act', lambda e: e.activation(out=rstd[:, :], in_=std[:, :], func=AF.Exp, scale=-0.5),
                     reads=[std_b], writes=[rstd_b])
                cx['rstd'], cx['rstd_b'] = rstd, rstd_b

            def normalize(cx):
                xt, xt_b, rstd, rstd_b, ts = cx['xt'], cx['xt_b'], cx['rstd'], cx['rstd_b'], cx['ts']
                ot, ot_b = orot.next()
                for kc in range(KC):
                    if b_ap is None:
                        k.op('dve', lambda e, kc=kc: e.scalar_tensor_tensor(
                            out=ot[:, kc, :], in0=xt[:, kc, :], scalar=a_ap[:, kc:kc + 1], in1=rstd[:, :],
                            op0=ALU.mult, op1=ALU.mult), reads=[xt_b, rstd_b], writes=[ot_b])
                    else:
                        tp, tp_b = trot.next()
                        k.op('dve', lambda e, tp=tp, kc=kc: e.scalar_tensor_tensor(
                            out=tp[:, :], in0=xt[:, kc, :], scalar=a_ap[:, kc:kc + 1], in1=rstd[:, :],
                            op0=ALU.mult, op1=ALU.mult), reads=[xt_b, rstd_b], writes=[tp_b])
                        k.op('act', lambda e, tp=tp, kc=kc: e.activation(
                            out=ot[:, kc, :], in_=tp[:, :], func=AF.Identity, bias=b_ap[:, kc:kc + 1]),
                            reads=[tp_b], writes=[ot_b])
                k.store('sp', dstv[:, :, ts], ot[:, :, :], ot_b)

            prev = None
            for tt in range(self.NT):
                cx = stats_a(tt)
                if prev is not None:
                    normalize(prev)
                stats_b(cx)
                prev = cx
            normalize(prev)

    def ph_norm(self, l, i):
        ao = (l * 3 + i) * 16
        base = l * 144 + i * 48
        self._ensure_eps()
        self._norm(self.x_cur, self.h, BF16, self.colsA[:, ao:ao + 16], self.mod[:, base:base + 16])

    def ph_fnorm(self, l):
        go, _ = SC['ng%d_3' % l]
        self._ensure_eps()
        dst = self.outT if l == DEPTH - 1 else self.x
        self._norm(self.x_cur, dst, F32, self.sc[:, go:go + 16], None)
        self.x_cur = dst

    def _ensure_eps(self):
        if not hasattr(self, 'epsc'):
            k = self.k
            self.epsc, self.epsc_b = k.sbb("epsc", [128, 1], F32)
            k.op('dve', lambda e: e.memset(self.epsc[:, :], EPS), writes=[self.epsc_b])

    def _gemm(self, wsrcs, kcn, col0, ncols, rhs_list, wrots, psrot, evac, gw=256):
        k = self.k
        c = 0
        j = 0
        while c < ncols:
            g = min(gw, ncols - c)
            slabs = []
            for wsrc, wrot in zip(wsrcs, wrots):
                wt, wt_b = wrot.next()
                k.dma('pool', wt[:, :, 0:g], wsrc[:, :, col0 + c:col0 + c + g], writes=[wt_b])
                slabs.append((wt, wt_b))
            cc = 0
            while cc < g:
                m = min(128, g - cc)
                for si, (rhs, rhs_b) in enumerate(rhs_list):
                    pss = []
                    for (wt, wt_b) in slabs:
                        ps, ps_b = psrot.next()

                        def grp(e, wt=wt, ps=ps, cc=cc, m=m, rhs=rhs):
                            ins = None
                            for kc in range(kcn):
                                ins = e.matmul(ps[0:m, :], lhsT=wt[:, kc, cc:cc + m], rhs=rhs[:, kc, :],
                                               start=(kc == 0), stop=(kc == kcn - 1))
                            return ins
                        k.op('pe', grp, reads=[wt_b, rhs_b], writes=[ps_b])
                        pss.append((ps, ps_b))
                    evac(j, c + cc, m, pss, si)
                j += 1
                cc += m
            c += g

    def ph_ffn(self, l, w):
        k = self.k
        i = 0 if w == 0 else 2
        go = (l * 3 + i) * 16
        hv = self.h.rearrange("(kc p) t -> p kc t", p=128)
        xsrc = self.x_cur.rearrange("(kc p) t -> p kc t", p=128)
        xdst = self.x.rearrange("(kc p) t -> p kc t", p=128)
        wgv = self.wg[l, w].rearrange("(kc p) n -> p kc n", p=128)
        wuv = self.wu[l, w].rearrange("(kc p) n -> p kc n", p=128)
        wdv = self.wd[l, w].rearrange("(kc p) n -> p kc n", p=128)
        with k.phase():
            hts = [k.sbb("fh%d" % i_, [128, KC, TT], BF16) for i_ in range(2)]
            acts = [k.sbb("fact%d" % i_, [128, FC, TT], BF16) for i_ in range(2)]
            wgrot = k.rot("fwg", 2, [128, KC, 128], BF16)
            wurot = k.rot("fwu", 2, [128, KC, 128], BF16)
            wdrot = k.rot("fwd", 2, [128, FC, 128], BF16)
            sgrot = k.rot("fsg", 2, [128, TT], F32)
            xorot = k.rot("fxo", 3, [128, TT], F32)
            xnrot = k.rot("fxn", 3, [128, TT], F32)
            psrot = Rot(self.pb[0:6])
            psrot2 = Rot(self.pb[6:8])
            for t2 in range(0, self.NT, 2):
                tts = list(range(t2, min(t2 + 2, self.NT)))
                subs = []
                tsl = []
                for si, tt in enumerate(tts):
                    ht, ht_b = hts[si]
                    k.load('sp', ht[:, :, :], ht_b, hv[:, :, tt * TT:(tt + 1) * TT])
                    subs.append((ht, ht_b))
                    tsl.append(slice(tt * TT, (tt + 1) * TT))

                def evac_gu(j, c, m, pss, si):
                    (pg, pg_b), (pu, pu_b) = pss
                    act, act_b = acts[si]
                    sg, sg_b = sgrot.next()
                    k.op('act', lambda e: e.activation(out=sg[:, :], in_=pg[:, :], func=AF.Silu),
                         reads=[pg_b], writes=[sg_b])
                    k.op('dve', lambda e: e.tensor_tensor(act[:, j, :], sg[:, :], pu[:, :], ALU.mult),
                         reads=[sg_b, pu_b], writes=[act_b])
                self._gemm([wgv, wuv], KC, 0, DFF, subs, [wgrot, wurot], psrot, evac_gu, gw=128)

                def evac_d(j, c, m, pss, si):
                    (po, po_b), = pss
                    xo, xo_b = xorot.next()
                    k.load('sp', xo[:, :], xo_b, xsrc[:, j, tsl[si]])
                    xn, xn_b = xnrot.next()
                    k.op('dve', lambda e: e.scalar_tensor_tensor(
                        out=xn[:, :], in0=po[:, :], scalar=self.colsG[:, go + j:go + j + 1], in1=xo[:, :],
                        op0=ALU.mult, op1=ALU.add), reads=[po_b, xo_b, self.colsG_b], writes=[xn_b])
                    k.store('sp', xdst[:, j, tsl[si]], xn[:, :], xn_b)
                self._gemm([wdv], FC, 0, D, [acts[si] for si in range(len(tts))], [wdrot], psrot2, evac_d, gw=128)
        self.x_cur = self.x

    def ph_mixin(self, l):
        k = self.k
        hv = self.h.rearrange("(kc p) t -> p kc t", p=128)
        wv = self.w_in[l].rearrange("(kc p) n -> p kc n", p=128)
        self._ensure_eps()
        with k.phase():
            hrot = k.rot("mh", 4, [128, KC, TT], BF16)
            wrot = k.rot("mw", 2, [128, KC, 256], BF16)
            wtrot = k.rot("mwt", 2, [128, KC, 512], BF16)
            o16rot = k.rot("mo16", 6, [128, TT], BF16)
            o32rot = k.rot("mo32", 3, [128, TT], F32)
            sqrot = k.rot("msq", 2, [128, TT], F32)
            strot = k.rot("mst", 2, [128, TT], F32)
            rsrot = k.rot("mrs", 2, [128, TT], F32)
            psrot = Rot(self.pb[0:5])
            psrot2 = Rot(self.pb[5:8])
            for t2 in range(0, self.NT, 2):
                tts = list(range(t2, min(t2 + 2, self.NT)))
                subs = []
                tsl = []
                for tt in tts:
                    ht, ht_b = hrot.next()
                    k.load('sp', ht[:, :, :], ht_b, hv[:, :, tt * TT:(tt + 1) * TT])
                    subs.append((ht, ht_b))
                    tsl.append(slice(tt * TT, (tt + 1) * TT))

                def ev_qknorm(dst, gcol):
                    def ev(j, c, m, pss, si):
                        (ps, ps_b), = pss
                        sq, sq_b = sqrot.next()
                        k.op('act', lambda e: e.activation(out=sq[:, :], in_=ps[:, :], func=AF.Square),
                             reads=[ps_b], writes=[sq_b])
                        p2, p2_b = psrot2.next()
                        k.op('pe', lambda e: e.matmul(p2[:, :], lhsT=self.cst[:, 128:256], rhs=sq[:, :],
                                                      start=True, stop=True),
                             reads=[sq_b, self.cst_b], writes=[p2_b])
                        st, st_b = strot.next()
                        k.op('act', lambda e: e.activation(out=st[:, :], in_=p2[:, :], func=AF.Ln,
                                                           bias=self.epsc[:, 0:1], scale=1.0 / 64),
                             reads=[p2_b], writes=[st_b])
                        rs, rs_b = rsrot.next()
                        k.op('act', lambda e: e.activation(out=rs[:, :], in_=st[:, :], func=AF.Exp, scale=-0.5), reads=[st_b], writes=[rs_b])
                        o, o_b = o16rot.next()
                        k.op('dve', lambda e: e.scalar_tensor_tensor(
                            out=o[:, :], in0=ps[:, :], scalar=gcol, in1=rs[:, :], op0=ALU.mult, op1=ALU.mult),
                            reads=[ps_b, rs_b, self.sc_b], writes=[o_b])
                        k.store('sp', dst[c:c + m, tsl[si]], o[0:m, :], o_b)
                    return ev

                def ev_act(dst, func):
                    def ev(j, c, m, pss, si):
                        (ps, ps_b), = pss
                        o, o_b = o16rot.next()
                        k.op('act', lambda e: e.activation(out=o[0:m, :], in_=ps[0:m, :], func=func),
                             reads=[ps_b], writes=[o_b])
                        k.store('sp', dst[c:c + m, tsl[si]], o[0:m, :], o_b)
                    return ev

                def ev_copy16(dst):
                    def ev(j, c, m, pss, si):
                        (ps, ps_b), = pss
                        o, o_b = o16rot.next()
                        k.op('dve', lambda e: e.tensor_copy(o[0:m, :], ps[0:m, :]), reads=[ps_b], writes=[o_b])
                        k.store('sp', dst[c:c + m, tsl[si]], o[0:m, :], o_b)
                    return ev

                def ev_copy32(dst):
                    def ev(j, c, m, pss, si):
                        (ps, ps_b), = pss
                        o, o_b = o32rot.next()
                        k.op('dve', lambda e: e.tensor_copy(o[0:m, :], ps[0:m, :]), reads=[ps_b], writes=[o_b])
                        k.store('sp', dst[c:c + m, tsl[si]], o[0:m, :], o_b)
                    return ev

                fm = [
                    (O_AQ, 1024, ev_qknorm(self.aq, self.col('qg%d' % l))),
                    (O_AK, 1024, ev_qknorm(self.ak, self.col('kg%d' % l))),
                    (O_BQ, 1024, ev_copy16(self.bq)),
                    (O_BF, 1024, ev_copy32(self.bz)),
                    (O_BG, 1024, ev_act(self.bsg, AF.Sigmoid)),
                    (O_CQ, 512, ev_copy16(self.cq)),
                    (O_CK, 512, ev_copy16(self.ck)),
                    (O_CR, 1024, ev_act(self.crs, AF.Silu)),
                    (O_CGD, 16, ev_copy16(self.cgd)),
                    (O_GATE, 3 * D, ev_act(self.gat, AF.Sigmoid)),
                ]
                for col0, ncols, ev in fm:
                    self._gemm([wv], KC, col0, ncols, subs, [wrot], psrot, ev)
                for vi, col0 in enumerate((O_AV, O_BI, O_CV)):
                    for g in range(2):
                        wt, wt_b = wtrot.next()
                        k.dma('pool', wt[:, :, :], wv[:, :, col0 + g * 512:col0 + (g + 1) * 512], writes=[wt_b])
                        for si, (ht, ht_b) in enumerate(subs):
                            for tb in range(4):
                                ps, ps_b = psrot.next()

                                def grp(e, wt=wt, ps=ps, tb=tb, ht=ht):
                                    ins = None
                                    for kc in range(KC):
                                        ins = e.matmul(ps[:, :], lhsT=ht[:, kc, tb * 128:(tb + 1) * 128],
                                                       rhs=wt[:, kc, :], start=(kc == 0), stop=(kc == KC - 1))
                                    return ins
                                k.op('pe', grp, reads=[wt_b, ht_b], writes=[ps_b])
                                o, o_b = o16rot.next()
                                if tb % 2 == 0:
                                    k.op('act', lambda e, o=o, ps=ps: e.activation(out=o[:, :], in_=ps[:, :],
                                                                                    func=AF.Identity),
                                         reads=[ps_b], writes=[o_b])
                                else:
                                    k.op('dve', lambda e, o=o, ps=ps: e.tensor_copy(o[:, :], ps[:, :]),
                                         reads=[ps_b], writes=[o_b])
                                r0 = tts[si] * TT + tb * 128
                                k.store('sp', self.vtok[vi][r0:r0 + 128, g * 512:(g + 1) * 512], o[:, :], o_b)

    def ph_attn(self, l):
        k = self.k
        T, NT = self.T, self.NT
        NKB = T // 128
        SCALE = 0.125
        ro, _ = SC['relb']
        self._ensure_eps()
        with k.phase():
            ktrot = k.rot("akT", 2, [128, T], BF16)
            vrot = k.rot("av", 2, [128, NKB, 128], BF16)
            qrot = k.rot("aq", 2, [128, TT], BF16)
            erot = [k.rot("ae%d" % m, 3, [128, TT], BF16) for m in range(2)]
            tmprot = k.rot("atmp", 4, [128, 128], F32)
            rrot = k.rot("ar", 2, [128, TT], F32)
            trot = k.rot("at", 2, [128, TT], F32)
            d_, d_b = k.sbb("ad", [128, TT], F32)
            sq, sq_b = k.sbb("asq", [128, TT], F32)
            st, st_b = k.sbb("ast", [128, TT], F32)
            rs, rs_b = k.sbb("ars", [128, TT], F32)
            yrot = k.rot("ay", 2, [128, TT], BF16)
            srot = Rot([(self.pb[0], self.pb[1]), (self.pb[2], self.pb[3])])
            (po0, po0_b), (po1, po1_b), (pz0, pz0_b), (pz1, pz1_b) = self.pb[4:8]
            pos = [(po0, po0_b), (po1, po1_b)]
            pzs = [(pz0, pz0_b), (pz1, pz1_b)]
            neg_lam = self.misc[:, l * 16:l * 16 + 1]
            gcol = self.misc[:, l * 16 + 1:l * 16 + 2]
            drot = k.rot("adr", 2, [128, TT], F32)
            pending_tail = []

            def post_tail(dt_, dt_b, h, Q):
                k.op('act', lambda e: e.activation(out=sq[:, :], in_=dt_[:, :], func=AF.Square),
                     reads=[dt_b], writes=[sq_b])
                (pn, pn_b) = srot.next()[0]
                k.op('pe', lambda e: e.matmul(pn[:, :], lhsT=self.ones32[:, :], rhs=sq[:, :], start=True, stop=True),
                     reads=[sq_b, self.ones32_b], writes=[pn_b])
                k.op('act', lambda e: e.activation(out=st[:, :], in_=pn[:, :], func=AF.Ln,
                                                   bias=self.epsc[:, 0:1], scale=1.0 / 128),
                     reads=[pn_b], writes=[st_b])
                k.op('act', lambda e: e.activation(out=rs[:, :], in_=st[:, :], func=AF.Exp, scale=-0.5),
                     reads=[st_b], writes=[rs_b])
                yt, yt_b = yrot.next()
                k.op('dve', lambda e: e.scalar_tensor_tensor(
                    out=yt[:, :], in0=dt_[:, :], scalar=gcol, in1=rs[:, :], op0=ALU.mult, op1=ALU.mult),
                    reads=[dt_b, rs_b, self.misc_b], writes=[yt_b])
                k.store('sp', self.y[0][h * 128:(h + 1) * 128, Q * TT:(Q + 1) * TT], yt[:, :], yt_b)

            for h in range(8):
                kt, kt_b = ktrot.next()
                k.load('sp', kt[:, :], kt_b, self.ak[h * 128:(h + 1) * 128, :])
                vt, vt_b = vrot.next()
                vsrc_h = self.vtok[0][:, h * 128:(h + 1) * 128].rearrange("(b p) d -> p b d", p=128)
                for b0 in range(0, NKB, 8):
                    b1 = min(NKB, b0 + 8)
                    k.dma('sp', vt[:, b0:b1, :], vsrc_h[:, b0:b1, :], writes=[vt_b])
                cb = self.sc[:, ro + 15 * 8 + h:ro + 15 * 8 + h + 1]
                for Q in range(NT):
                    qt, qt_b = qrot.next()
                    k.load('sp', qt[:, :], qt_b, self.aq[h * 128:(h + 1) * 128, Q * TT:(Q + 1) * TT])
                    nkb = 4 * Q + 4

                    def emit_S(kb):
                        c0_ = max(0, kb - 4 * Q) * 128
                        (s0, s1) = srot.next()
                        pair = [s0, s1]
                        for m in range(2):
                            ps, ps_b = pair[m]
                            k.op('pe', lambda e, ps=ps, m=m, kb=kb, c0_=c0_, kt=kt, qt=qt: e.matmul(
                                ps[:, c0_:TT], lhsT=kt[m * 64:(m + 1) * 64, kb * 128:(kb + 1) * 128],
                                rhs=qt[m * 64:(m + 1) * 64, c0_:TT], start=True, stop=True),
                                reads=[kt_b, qt_b], writes=[ps_b])
                        return pair
                    nxt = emit_S(0)
                    for kb in range(nkb):
                        i0 = max(0, kb - 4 * Q)
                        c0 = i0 * 128
                        sp_ = nxt
                        if kb + 1 < nkb:
                            nxt = emit_S(kb + 1)
                        es = []
                        for m in range(2):
                            ps, ps_b = sp_[m]
                            et, et_b = erot[m].next()
                            cplain = c0
                            for typ, i in ((0, kb - 4 * Q), (1, kb - 4 * Q + 1)):
                                if 0 <= i <= 3:
                                    tp, tp_b = tmprot.next()
                                    k.op('dve', lambda e, tp=tp, ps=ps, i=i, typ=typ, h=h: e.scalar_tensor_tensor(
                                        out=tp[:, :], in0=ps[:, i * 128:(i + 1) * 128], scalar=SCALE,
                                        in1=self.biasT[:, h, typ, :], op0=ALU.mult, op1=ALU.add),
                                        reads=[ps_b, self.biasT_b], writes=[tp_b])
                                    k.op('act', lambda e, tp=tp, et=et, i=i: e.activation(
                                        out=et[:, i * 128:(i + 1) * 128], in_=tp[:, :], func=AF.Exp),
                                        reads=[tp_b], writes=[et_b])
                                    cplain = max(cplain, (i + 1) * 128)
                            if cplain < TT:
                                k.op('act', lambda e, et=et, ps=ps, cplain=cplain, cb=cb: e.activation(
                                    out=et[:, cplain:TT], in_=ps[:, cplain:TT], func=AF.Exp, bias=cb, scale=SCALE),
                                    reads=[ps_b, self.sc_b], writes=[et_b])
                            es.append((et, et_b))
                        for m in range(2):
                            et, et_b = es[m]
                            po, po_b = pos[m]
                            pz, pz_b = pzs[m]
                            k.op('pe', lambda e, po=po, et=et, kb=kb, c0=c0, vt=vt, nkb=nkb: e.matmul(
                                po[:, c0:TT], lhsT=vt[:, kb, :], rhs=et[:, c0:TT], start=(kb == 0),
                                stop=(kb == nkb - 1)), reads=[vt_b, et_b], writes=[po_b])
                            k.op('pe', lambda e, pz=pz, et=et, kb=kb, c0=c0, nkb=nkb: e.matmul(
                                pz[:, c0:TT], lhsT=self.ones16[:, :], rhs=et[:, c0:TT], start=(kb == 0),
                                stop=(kb == nkb - 1)), reads=[self.ones16_b, et_b], writes=[pz_b])
                    ts_ = []
                    for m in range(2):
                        r, r_b = rrot.next()
                        k.op('dve', lambda e, r=r, m=m: e.reciprocal(r[:, :], pzs[m][0][:, :]),
                             reads=[pzs[m][1]], writes=[r_b])
                        t_, t_b = trot.next()
                        k.op('dve', lambda e, t_=t_, r=r, m=m: e.tensor_tensor(t_[:, :], pos[m][0][:, :], r[:, :],
                                                                             ALU.mult),
                             reads=[pos[m][1], r_b], writes=[t_b])
                        ts_.append((t_, t_b))
                    dt_, dt_b = drot.next()
                    k.op('dve', lambda e, a=ts_[0][0], b=ts_[1][0], dt_=dt_: e.scalar_tensor_tensor(
                        out=dt_[:, :], in0=b[:, :], scalar=neg_lam, in1=a[:, :], op0=ALU.mult, op1=ALU.add),
                        reads=[ts_[0][1], ts_[1][1], self.misc_b], writes=[dt_b])
                    if pending_tail:
                        post_tail(*pending_tail.pop(0))
                    pending_tail.append((dt_, dt_b, h, Q))
            while pending_tail:
                post_tail(*pending_tail.pop(0))

    def ph_gla(self, l, kind):
        k = self.k
        NT = self.NT
        isB = (kind == 'B')
        H = 8 if isB else 4
        DV = 128 if isB else 256
        NV = DV // 128
        vsrc = self.vtok[1] if isB else self.vtok[2]
        ydst = self.y[1] if isB else self.y[2]
        self._ensure_eps()
        with k.phase():
            S32 = [[k.sbb("gS32_%d_%d" % (h, i), [128, DV], F32) for i in range(2)] for h in range(H)]
            for h in range(H):
                k.op('dve', lambda e, h=h: e.memset(S32[h][0][0][:, :], 0.0), writes=[S32[h][0][1]])
            vrot = k.rot("gv", 2, [32, 16, MIX], BF16)
            cgrot = k.rot("gcg", 2, [16, TT], BF16)
            qrot = k.rot("gq", 2, [128, TT], BF16)
            kkrot = k.rot("gkk", 2, [128, TT], F32 if isB else BF16)
            zrot = k.rot("gz", 2, [128, TT], F32)
            gaterot = k.rot("ggate", 2, [128, NV, TT], BF16)
            f_, f_b = k.sbb("gf", [128, TT], F32)
            g_, g_b = k.sbb("gg", [128, TT], F32)
            b_, b_b = k.sbb("gb", [128, TT], F32)
            nb, nb_b = k.sbb("gnb", [128, TT], F32)
            ek, ek_b = k.sbb("gek", [128, TT], F32)
            ebrot = k.rot("geb", 2, [128, TT], F32)
            qhrot = k.rot("gqh", 2, [128, TT], BF16)
            ktlrot = k.rot("gkt", 2, [128, TT], BF16)
            PTrot = k.rot("gPT", 2, [32, TT], BF16)
            ktokrot = k.rot("gktok", 2, [32, 16, 128], BF16)
            tdall = [k.sb("gtd%d" % i, [128, 16, DV], F32) for i in range(2)]
            tdall_b = [[Buf("gtd%d_%d" % (i, c)) for c in range(16)] for i in range(2)]
            s16all = [k.sb("gs16_%d" % i, [128, 16, DV], BF16) for i in range(2)]
            s16all_b = [[Buf("gs16_%d_%d" % (i, c)) for c in range(16)] for i in range(2)]
            t1rot = k.rot("gt1", 2 * NV, [128, TT], F32)
            sqrot = k.rot("gsq", 2, [128, TT], F32)
            st, st_b = k.sbb("gst", [128, TT], F32)
            rs, rs_b = k.sbb("grs", [128, TT], F32)
            yt_rot = k.rot("gy", 2, [128, NV, TT], BF16)
            (pss, pss_b) = self.pb[0]
            (ptp, ptp_b) = self.pb[1]
            porot = Rot([(self.pb[2], self.pb[3]), (self.pb[4], self.pb[5])])
            pdrot = Rot([self.pb[6], self.pb[7]])
            rmask = self.cst[:, 256:768]
            cmask = self.cst[0:32, 768:1280]
            qscale = 1.0 if isB else 128.0 ** -0.5
            lbo = 48 if l == 0 else 32
            omo = 56 if l == 0 else 40
            slot = [0]

            def pre(tt, h, vt, vt_b, cg, cg_b):
                ts = slice(tt * TT, (tt + 1) * TT)
                hs = slice(h * 128, (h + 1) * 128)
                si = slot[0] % 2
                slot[0] += 1
                qt, qt_b = qrot.next()
                k.load('sp', qt[:, :], qt_b, (self.bq if isB else self.cq)[hs, ts])
                gt, gt_b = gaterot.next()
                if isB:
                    k.load('sp', gt[:, 0, :], gt_b, self.bsg[hs, ts])
                else:
                    k.load('sp', gt[:, :, :], gt_b,
                           self.crs[h * 256:(h + 1) * 256, ts].rearrange("(j p) t -> p j t", p=128))
                kk, kk_b = kkrot.next()
                if isB:
                    zt, zt_b = zrot.next()
                    k.load('sp', zt[:, :], zt_b, self.bz[hs, ts])
                    k.op('act', lambda e: e.activation(out=f_[:, :], in_=zt[:, :], func=AF.Sigmoid),
                         reads=[zt_b], writes=[f_b])
                    k.op('dve', lambda e: e.tensor_scalar(
                        f_[:, :], f_[:, :], self.misc[:, omo + h:omo + h + 1], self.misc[:, lbo + h:lbo + h + 1],
                        op0=ALU.mult, op1=ALU.add), reads=[f_b, self.misc_b], writes=[f_b])
                    k.op('act', lambda e: e.activation(out=g_[:, :], in_=f_[:, :], func=AF.Ln),
                         reads=[f_b], writes=[g_b])
                    k.op('dve', lambda e: e.tensor_scalar(kk[:, :], f_[:, :], -1.0, 1.0, op0=ALU.mult, op1=ALU.add),
                         reads=[f_b], writes=[kk_b])
                else:
                    (pm, pm_b) = pdrot.next()
                    k.op('pe', lambda e: e.matmul(
                        pm[:, :], lhsT=self.wup16[:, l * 512 + h * 128:l * 512 + (h + 1) * 128], rhs=cg[:, :],
                        start=True, stop=True), reads=[self.wup16_b, cg_b], writes=[pm_b])
                    k.op('act', lambda e: e.activation(out=f_[:, :], in_=pm[:, :], func=AF.Sigmoid,
                                                       bias=self.col('glab%d' % l, h)),
                         reads=[pm_b, self.sc_b], writes=[f_b])
                    k.op('act', lambda e: e.activation(out=g_[:, :], in_=f_[:, :], func=AF.Ln),
                         reads=[f_b], writes=[g_b])
                    k.op('dve', lambda e: e.tensor_scalar(g_[:, :], g_[:, :], 1.0 / 16.0, None, op0=ALU.mult),
                         reads=[g_b], writes=[g_b])
                    k.load('sp', kk[:, :], kk_b, self.ck[hs, ts])
                eb, eb_b = ebrot.next()
                qh, qh_b = qhrot.next()
                ktl, ktl_b = ktlrot.next()
                PT, PT_b = PTrot.next()
                ktok, ktok_b = ktokrot.next()
                k.op('dve', lambda e: e.tensor_tensor_scan(out=b_[:, :], data0=rmask, data1=g_[:, :],
                                                           initial=0.0, op0=ALU.mult, op1=ALU.add),
                     reads=[g_b, self.cst_b], writes=[b_b])
                k.op('act', lambda e: e.activation(out=eb[:, :], in_=b_[:, :], func=AF.Exp),
                     reads=[b_b], writes=[eb_b])
                k.op('dve', lambda e: e.tensor_scalar(nb[:, :], b_[:, :], -1.0, 80.0, op0=ALU.mult, op1=ALU.min),
                     reads=[b_b], writes=[nb_b])
                k.op('act', lambda e: e.activation(out=ek[:, :], in_=nb[:, :], func=AF.Exp),
                     reads=[nb_b], writes=[ek_b])
                k.op('dve', lambda e: e.scalar_tensor_tensor(
                    out=qh[:, :], in0=qt[:, :], scalar=qscale, in1=eb[:, :], op0=ALU.mult, op1=ALU.mult),
                    reads=[qt_b, eb_b], writes=[qh_b])
                k.op('dve', lambda e: e.tensor_tensor(ktl[:, :], kk[:, :], ek[:, :], ALU.mult),
                     reads=[kk_b, ek_b], writes=[ktl_b])

                def sc_grp(e):
                    ins = None
                    for c in range(16):
                        cs = slice(c * 32, (c + 1) * 32)
                        ins = e.matmul(pss[0:32, cs], lhsT=ktl[:, cs], rhs=qh[:, cs], start=True, stop=True)
                    return ins
                k.op('pe', sc_grp, reads=[ktl_b, qh_b], writes=[pss_b])
                k.op('dve', lambda e: e.tensor_tensor(PT[:, :], pss[0:32, :], cmask, ALU.mult),
                     reads=[pss_b, self.cst_b], writes=[PT_b])
                ptv = ptp[:, :].bitcast(BF16)
                for half in range(2):
                    def tr_grp(e, half=half):
                        ins = None
                        for c8 in range(8):
                            c = half * 8 + c8
                            ins = e.transpose(ptv[0:32, c8 * 128:(c8 + 1) * 128], ktl[:, c * 32:(c + 1) * 32],
                                              self.ident16[:, :])
                        return ins
                    k.op('pe', tr_grp, reads=[ktl_b, self.ident16_b], writes=[ptp_b])
                    k.op('act', lambda e, half=half: e.activation(
                        out=ktok[:, half * 8:(half + 1) * 8, :].rearrange("p c d -> p (c d)"),
                        in_=ptv[0:32, :], func=AF.Identity), reads=[ptp_b], writes=[ktok_b])
                td = tdall[si]
                for c in range(16):
                    pd, pd_b = pdrot.next()
                    dec = eb[:, c * 32 + 31:c * 32 + 32]
                    k.op('pe', lambda e, pd=pd, c=c: e.matmul(
                        pd[:, 0:DV], lhsT=ktok[:, c, :], rhs=vt[0:32, c, h * DV:(h + 1) * DV],
                        start=True, stop=True), reads=[ktok_b, vt_b], writes=[pd_b])
                    k.op('act', lambda e, pd=pd, c=c, dec=dec: e.activation(
                        out=td[:, c, :], in_=pd[:, 0:DV], func=AF.Identity, scale=dec),
                        reads=[pd_b, eb_b], writes=[tdall_b[si][c]])
                return dict(tt=tt, h=h, si=si, vt=vt, vt_b=vt_b, gt=gt, gt_b=gt_b, eb=eb, eb_b=eb_b, qh=qh, qh_b=qh_b,
                            PT=PT, PT_b=PT_b)

            def chain(cx):
                h, si, eb, eb_b = cx['h'], cx['si'], cx['eb'], cx['eb_b']
                td = tdall[si]
                s16 = s16all[si]
                for c in range(16):
                    (sa, sa_b) = S32[h][c % 2]
                    (sn, sn_b) = S32[h][(c + 1) % 2]
                    dec = eb[:, c * 32 + 31:c * 32 + 32]
                    k.op('act', lambda e, c=c, sa=sa: e.activation(out=s16[:, c, :], in_=sa[:, :], func=AF.Identity),
                         reads=[sa_b], writes=[s16all_b[si][c]])
                    k.op('dve', lambda e, c=c, sa=sa, sn=sn, dec=dec: e.scalar_tensor_tensor(
                        out=sn[:, :], in0=sa[:, :], scalar=dec, in1=td[:, c, :], op0=ALU.mult, op1=ALU.add),
                        reads=[sa_b, tdall_b[si][c], eb_b], writes=[sn_b])

            def out(cx):
                tt, h, si = cx['tt'], cx['h'], cx['si']
                vt, vt_b, gt, gt_b = cx['vt'], cx['vt_b'], cx['gt'], cx['gt_b']
                qh, qh_b, PT, PT_b = cx['qh'], cx['qh_b'], cx['PT'], cx['PT_b']
                ts = slice(tt * TT, (tt + 1) * TT)
                s16 = s16all[si]
                pos = porot.next()
                for c in range(16):
                    cs = slice(c * 32, (c + 1) * 32)
                    for j in range(NV):
                        po, po_b = pos[j]

                        def o_grp(e, po=po, j=j, c=c, cs=cs):
                            e.matmul(po[:, cs], lhsT=vt[0:32, c, h * DV + j * 128:h * DV + (j + 1) * 128],
                                     rhs=PT[:, cs], start=True, stop=False)
                            return e.matmul(po[:, cs], lhsT=s16[:, c, j * 128:(j + 1) * 128], rhs=qh[:, cs],
                                            start=False, stop=True)
                        k.op('pe', o_grp, reads=[vt_b, PT_b, s16all_b[si][c], qh_b], writes=[po_b])
                yt, yt_b = yt_rot.next()
                (pm, pm_b) = pdrot.next()
                t1s = []
                for j in range(NV):
                    po, po_b = pos[j]
                    t1, t1_b = t1rot.next()
                    if isB:
                        k.op('dve', lambda e, t1=t1, po=po: e.tensor_tensor(t1[:, :], po[:, :], gt[:, 0, :], ALU.mult),
                             reads=[po_b, gt_b], writes=[t1_b])
                    else:
                        k.op('act', lambda e, t1=t1, po=po: e.activation(out=t1[:, :], in_=po[:, :], func=AF.Identity),
                             reads=[po_b], writes=[t1_b])
                    sq, sq_b = sqrot.next()
                    k.op('act', lambda e, sq=sq, t1=t1: e.activation(out=sq[:, :], in_=t1[:, :], func=AF.Square),
                         reads=[t1_b], writes=[sq_b])
                    k.op('pe', lambda e, sq=sq, j=j: e.matmul(pm[:, :], lhsT=self.ones32[:, :], rhs=sq[:, :],
                                                              start=(j == 0), stop=(j == NV - 1)),
                         reads=[sq_b, self.ones32_b], writes=[pm_b])
                    t1s.append((t1, t1_b))
                k.op('act', lambda e: e.activation(out=st[:, :], in_=pm[:, :], func=AF.Ln,
                                                   bias=self.epsc[:, 0:1], scale=1.0 / DV),
                     reads=[pm_b], writes=[st_b])
                k.op('act', lambda e: e.activation(out=rs[:, :], in_=st[:, :], func=AF.Exp, scale=-0.5), reads=[st_b], writes=[rs_b])
                for j in range(NV):
                    t1, t1_b = t1s[j]
                    if isB:
                        k.op('dve', lambda e, t1=t1, j=j: e.scalar_tensor_tensor(
                            out=yt[:, j, :], in0=t1[:, :], scalar=self.col('hog%d' % l), in1=rs[:, :],
                            op0=ALU.mult, op1=ALU.mult), reads=[t1_b, rs_b, self.sc_b], writes=[yt_b])
                    else:
                        k.op('dve', lambda e, t1=t1, j=j: e.scalar_tensor_tensor(
                            out=t1[:, :], in0=t1[:, :], scalar=self.col('glag%d' % l, j), in1=rs[:, :],
                            op0=ALU.mult, op1=ALU.mult), reads=[t1_b, rs_b, self.sc_b], writes=[t1_b])
                        k.op('dve', lambda e, t1=t1, j=j: e.tensor_tensor(yt[:, j, :], t1[:, :], gt[:, j, :], ALU.mult),
                             reads=[t1_b, gt_b], writes=[yt_b])
                k.store('sp', ydst[h * DV:(h + 1) * DV, ts].rearrange("(j p) t -> p j t", p=128), yt[:, :, :], yt_b)

            pending = None
            for tt in range(NT):
                vt, vt_b = vrot.next()
                k.load('sp', vt[:, :, :], vt_b, vsrc[tt * TT:(tt + 1) * TT, :].rearrange("(c p) n -> p c n", p=32))
                cg = cg_b = None
                if not isB:
                    cg, cg_b = cgrot.next()
                    k.load('sp', cg[:, :], cg_b, self.cgd[:, tt * TT:(tt + 1) * TT])
                for h in range(H):
                    cx = pre(tt, h, vt, vt_b, cg, cg_b)
                    chain(cx)
                    if pending is not None:
                        out(pending)
                    pending = cx
            out(pending)

    def ph_merge(self, l):
        k = self.k
        go = (l * 3 + 1) * 16
        xsrc = self.x_cur.rearrange("(kc p) t -> p kc t", p=128)
        xdst = self.x.rearrange("(kc p) t -> p kc t", p=128)
        gv = self.gat.rearrange("(n kc p) t -> p n kc t", p=128, kc=KC)
        wbv = [self.w_br[l, n].rearrange("(kc p) d -> p kc d", p=128) for n in range(3)]
        wov = self.w_out[l].rearrange("(kc p) d -> p kc d", p=128)
        with k.phase():
            yrot = k.rot("my", 4, [128, 8, TT], BF16)
            wbrot = k.rot("mwb", 3, [128, 8, 256], BF16)
            worot = k.rot("mwo", 2, [128, KC, 256], BF16)
            grot = k.rot("mg", 4, [128, TT], BF16)
            maccs = [k.sbb("macc%d" % i_, [128, KC, TT], F32) for i_ in range(2)]
            m16s = [k.sbb("m16_%d" % i_, [128, KC, TT], BF16) for i_ in range(2)]
            tmprot = k.rot("mtmp", 2, [128, TT], F32)
            xorot = k.rot("mxo", 3, [128, TT], F32)
            xnrot = k.rot("mxn", 3, [128, TT], F32)
            psrot = Rot(self.pb[0:5])
            psrot2 = Rot(self.pb[5:8])
            for t2 in range(0, self.NT, 2):
                tts = list(range(t2, min(t2 + 2, self.NT)))
                tsl = [slice(tt * TT, (tt + 1) * TT) for tt in tts]
                for n in range(3):
                    subs = []
                    for si, tt in enumerate(tts):
                        yt, yt_b = yrot.next()
                        k.load('sp', yt[:, :, :], yt_b,
                               self.y[n].rearrange("(kc p) t -> p kc t", p=128)[:, :, tsl[si]])
                        subs.append((yt, yt_b))

                    def ev(j, c, m, pss, si, n=n):
                        (ps, ps_b), = pss
                        macc, macc_b = maccs[si]
                        m16, m16_b = m16s[si]
                        g, g_b = grot.next()
                        k.load('sp', g[:, :], g_b, gv[:, n, j, tsl[si]])
                        if n == 0:
                            k.op('dve', lambda e: e.tensor_tensor(macc[:, j, :], ps[:, :], g[:, :], ALU.mult),
                                 reads=[ps_b, g_b], writes=[macc_b])
                        else:
                            tp, tp_b = tmprot.next()
                            k.op('dve', lambda e: e.tensor_tensor(tp[:, :], ps[:, :], g[:, :], ALU.mult),
                                 reads=[ps_b, g_b], writes=[tp_b])
                            if n == 1:
                                k.op('dve', lambda e: e.tensor_tensor(macc[:, j, :], macc[:, j, :], tp[:, :], ALU.add),
                                     reads=[tp_b, macc_b], writes=[macc_b])
                            else:
                                k.op('dve', lambda e: e.tensor_tensor(m16[:, j, :], macc[:, j, :], tp[:, :], ALU.add),
                                     reads=[tp_b, macc_b], writes=[m16_b])
                    self._gemm([wbv[n]], 8, 0, D, subs, [wbrot], psrot, ev)

                def evo(j, c, m, pss, si):
                    (po, po_b), = pss
                    xo, xo_b = xorot.next()
                    k.load('sp', xo[:, :], xo_b, xsrc[:, j, tsl[si]])
                    xn, xn_b = xnrot.next()
                    k.op('dve', lambda e: e.scalar_tensor_tensor(
                        out=xn[:, :], in0=po[:, :], scalar=self.colsG[:, go + j:go + j + 1], in1=xo[:, :],
                        op0=ALU.mult, op1=ALU.add), reads=[po_b, xo_b, self.colsG_b], writes=[xn_b])
                    k.store('sp', xdst[:, j, tsl[si]], xn[:, :], xn_b)
                self._gemm([wov], KC, 0, D, [m16s[si] for si in range(len(tts))], [worot], psrot2, evo)
        self.x_cur = self.x


WEIGHT_KEYS = ['w_ada', 'ffn_w_gate', 'ffn_w_up', 'ffn_w_down', 'w_in', 'gla_w_gate_up', 'w_branch', 'w_out']


def make_in_maps(inputs, ncores, T):
    oht, cst = host_consts()
    maps = []
    shared = {kname: np.ascontiguousarray(np.asarray(inputs[kname], dtype=np.float32)) for kname in WEIGHT_KEYS}
    npi = {kk: np.asarray(v) for kk, v in inputs.items()}
    for b in range(ncores):
        m = dict(shared)
        m['xT'] = np.ascontiguousarray(npi['x'][b, :T].T.astype(np.float32))
        m['smallcols'] = host_smallcols(b, npi)
        m['cst'] = cst
        m['oht'] = oht
        maps.append(m)
    return maps


def kernel(**inputs):
    x = np.asarray(inputs['x'])
    B, T, _ = x.shape
    prog = Prog(T)
    nc = prog.build()
    maps = make_in_maps(inputs, B, T)
    res = run_bass_kernel_spmd(nc, maps, core_ids=list(range(B)))
    out = np.stack([np.ascontiguousarray(r['outT'].T) for r in res.results], axis=0)
    return out.astype(np.float32)
```

```python
import contextlib
import math
import numpy as np
import concourse.bass as bass
import concourse.mybir as mybir
from concourse.bass_utils import run_bass_kernel_spmd

F32 = mybir.dt.float32
BF16 = mybir.dt.bfloat16
AF = mybir.ActivationFunctionType
ALU = mybir.AluOpType
AX = mybir.AxisListType

ENGS = ['pe', 'act', 'dve', 'pool', 'sp']
COMPUTE = ['pe', 'act', 'dve', 'pool']

D = 2048
KC = 16
DFF = 5632
FC = 44
TT = 512
DIN = 16400
MIX = 1024
EPS = 1e-6
DEPTH = 2

O_AQ, O_AK, O_AV = 0, 1024, 2048
O_BQ, O_BF, O_BI, O_BG = 3072, 4096, 5120, 6144
O_CQ, O_CK, O_CV, O_CR, O_CGD, O_GATE = 7168, 7680, 8192, 9216, 10240, 10256


class Buf:
    __slots__ = ('name', 'w', 'r', 'dsem')

    def __init__(self, name):
        self.name = name
        self.w = None
        self.r = {}
        self.dsem = None


class DSem:
    __slots__ = ('sem', 'count', 'key')

    def __init__(self, sem, key):
        self.sem = sem
        self.count = 0
        self.key = key


class Rot:
    def __init__(self, items):
        self.items = items
        self.i = 0

    def next(self):
        it = self.items[self.i % len(self.items)]
        self.i += 1
        return it


class KB:
    def __init__(self):
        self.nc = bass.Bass("TRN2", target_bir_lowering=False)
        self.gstack = contextlib.ExitStack()
        self.stacks = [self.gstack]
        self.ops = {e: [] for e in ENGS}
        self.cnt = {e: 0 for e in ENGS}
        self.waited = {e: {} for e in ENGS}
        self.prog = {}
        self.dsems = []
        self.free_dsems = []
        self.dbufs = []
        self.nwaits = 0
        self.uid = 0
        self.strict = True
        for e in COMPUTE:
            self.prog[e] = self.gstack.enter_context(self.nc.semaphore('pg_' + e))

    def sb(self, name, shape, dt):
        self.uid += 1
        return self.stacks[-1].enter_context(
            self.nc.sbuf_tensor("%s_%d" % (name, self.uid), list(shape), dt))

    def sbb(self, name, shape, dt):
        return self.sb(name, shape, dt), Buf(name)

    def rot(self, name, n, shape, dt):
        return Rot([self.sbb("%s%d" % (name, i), shape, dt) for i in range(n)])

    def dram(self, name, shape, dt, kind="Internal"):
        return self.nc.dram_tensor(name, list(shape), dt, kind=kind).ap()

    def _dsem_for(self, b):
        if b.dsem is None:
            if self.free_dsems:
                b.dsem = self.free_dsems.pop()
            else:
                key = 'd%d' % len(self.dsems)
                sem = self.gstack.enter_context(self.nc.semaphore('ds%d' % len(self.dsems)))
                b.dsem = DSem(sem, key)
                self.dsems.append(b.dsem)
            self.dbufs.append(b)
        return b.dsem

    @contextlib.contextmanager
    def phase(self):
        st = contextlib.ExitStack()
        self.stacks.append(st)
        try:
            yield
        finally:
            self.barrier()
            self.stacks.pop()
            st.close()

    def _collect(self, eng, reads, writes, extra, is_dma=False):
        deps = {}

        def add(t):
            if t is None:
                return
            key, sem, val, peng = t
            if peng == eng and not is_dma and eng != 'pool' and not (self.strict and eng != 'pe'):
                return
            if key not in deps or deps[key][2] < val:
                deps[key] = t
        for b in reads:
            add(b.w)
        for b in writes:
            add(b.w)
            for t in b.r.values():
                add(t)
        for t in extra:
            add(t)
        out = []
        wd = self.waited[eng]
        for key, t in deps.items():
            if wd.get(key, 0) >= t[2]:
                continue
            wd[key] = t[2]
            out.append((t[1], t[2]))
        self.nwaits += len(out)
        return out

    def _update(self, tok, reads, writes):
        for b in writes:
            b.w = tok
            b.r = {}
        k = tok[0]
        for b in reads:
            if k not in b.r or b.r[k][2] < tok[2]:
                b.r[k] = tok

    def op(self, eng, fn, reads=(), writes=(), extra=()):
        waits = self._collect(eng, reads, writes, extra)
        self.cnt[eng] += 1
        tok = (eng, self.prog[eng], self.cnt[eng], eng)
        self.ops[eng].append((fn, waits, (self.prog[eng], 1)))
        self._update(tok, reads, writes)
        return tok

    def dma(self, q, out, in_, reads=(), writes=(), sembuf=None):
        waits = self._collect(q, reads, writes, (), is_dma=True)
        ds = self._dsem_for(sembuf if sembuf is not None else writes[0])
        ds.count += 16
        tok = (ds.key, ds.sem, ds.count, 'dma')

        def fn(e, out=out, in_=in_):
            return e.dma_start(out=out, in_=in_)
        self.ops[q].append((fn, waits, (ds.sem, 16)))
        self._update(tok, reads, writes)
        return tok

    def load(self, q, dst, dbuf, src):
        return self.dma(q, dst, src, reads=(), writes=[dbuf])

    def store(self, q, dst, src, sbuf_):
        return self.dma(q, dst, src, reads=[sbuf_], writes=(), sembuf=sbuf_)

    def barrier(self):
        toks = [(ds.sem, ds.count, ds.key) for ds in self.dsems if ds.count > 0]
        toks += [(self.prog[e], self.cnt[e], e) for e in COMPUTE if self.cnt[e] > 0]
        for e in ENGS:
            waits = []
            wd = self.waited[e]
            for sem, val, key in toks:
                if wd.get(key, 0) >= val:
                    continue
                wd[key] = val
                waits.append((sem, val))
            if waits:
                self.ops[e].append((None, waits, None))
        for b in self.dbufs:
            b.dsem = None
        self.dbufs = []
        self.free_dsems = list(self.dsems)

    def emit(self):
        self.barrier()
        nc = self.nc
        ops = self.ops

        def replay(name, e):
            for fn, waits, inc in ops[name]:
                for sem, val in waits:
                    e.wait_ge(sem, val)
                if fn is None:
                    continue
                ins = fn(e)
                if inc is not None:
                    ins.then_inc(inc[0], inc[1])

        with nc.Block() as block:
            @block.sync
            def _(e):
                replay('sp', e)

            @block.scalar
            def _(e):
                replay('act', e)

            @block.vector
            def _(e):
                replay('dve', e)

            @block.gpsimd
            def _(e):
                replay('pool', e)

            @block.tensor
            def _(e):
                replay('pe', e)
        self.gstack.close()
        return nc


def _t5_bucket_table():
    import jax
    import jax.numpy as jnp

    def t5_bucket(rel):
        half = 16
        max_exact = 8
        ret = jnp.where(rel > 0, half, 0)
        n = jnp.abs(rel)
        nf = jnp.maximum(n, 1).astype(jnp.float32)
        large = max_exact + (jnp.log(nf / max_exact) / math.log(128 / max_exact)
                             * (half - max_exact)).astype(jnp.int32)
        large = jnp.minimum(large, half - 1)
        return ret + jnp.where(n < max_exact, n, large)
    rel = np.arange(-255, 128, dtype=np.int32)
    try:
        cpu = jax.devices('cpu')[0]
        with jax.default_device(cpu):
            out = np.asarray(jax.jit(t5_bucket)(jnp.asarray(rel)))
    except Exception:
        n = np.abs(rel)
        nf = np.maximum(n, 1).astype(np.float32)
        large = 8 + (np.log(nf / np.float32(8)) / np.float32(math.log(16.0)) * np.float32(8)).astype(np.int32)
        large = np.minimum(large, 15)
        out = np.where(rel > 0, 16, 0) + np.where(n < 8, n, large)
    return {int(r): int(b) for r, b in zip(rel, out)}


def host_consts():
    bt = _t5_bucket_table()
    kk = np.arange(128)[:, None]
    qq = np.arange(128)[None, :]
    oht = np.zeros((2, 128, 33, 128), np.float32)
    for typ, off in ((0, 0), (1, -128)):
        rel = kk - qq + off
        bk = np.vectorize(lambda r: bt[int(r)])(rel)
        for b in range(32):
            oht[typ, :, b, :] = (bk == b)
    oht[0, :, 32, :] = np.where((kk // 64) <= (qq // 64), 0.0, -30000.0)
    cst = np.zeros((128, 128 + 128 + 512 + 512), np.float32)
    cst[:, 0:128] = np.eye(128)
    bd = np.zeros((128, 128), np.float32)
    bd[:64, :64] = 1.0
    bd[64:, 64:] = 1.0
    cst[:, 128:256] = bd
    rm = np.ones((128, 512), np.float32)
    rm[:, 0::32] = 0.0
    cst[:, 256:768] = rm
    s = np.arange(32)[:, None]
    t = np.arange(32)[None, :]
    cm = (s <= t).astype(np.float32)
    cst[:32, 768:1280] = np.tile(cm, (1, 16))
    return oht.reshape(2, 128, 33 * 128), cst


SC = {}


def _sc_layout():
    off = 0

    def add(name, n):
        nonlocal off
        SC[name] = (off, n)
        off += n
    add('c', 16)
    for l in range(DEPTH):
        add('bada%d' % l, 144)
        for i in range(4):
            add('ng%d_%d' % (l, i), 16)
        add('qg%d' % l, 1)
        add('kg%d' % l, 1)
        add('dog%d' % l, 1)
        add('hog%d' % l, 1)
        add('glab%d' % l, 4)
        add('glag%d' % l, 2)
        add('dl%d' % l, 256)
    add('lbl', 16)
    add('relb', 256)
    return off


NSC = _sc_layout()


def host_smallcols(b, inp):
    sc = np.zeros((128, NSC), np.float32)

    def put(name, arr):
        o, n = SC[name]
        sc[:, o:o + n] = arr
    put('c', inp['c'][b].reshape(16, 128).T)
    for l in range(DEPTH):
        put('bada%d' % l, inp['b_ada'][l].reshape(144, 128).T)
        for i in range(4):
            put('ng%d_%d' % (l, i), inp['norm_gains'][l, i].reshape(16, 128).T)
        put('qg%d' % l, np.tile(inp['qk_gains'][l, 0], 2)[:, None])
        put('kg%d' % l, np.tile(inp['qk_gains'][l, 1], 2)[:, None])
        put('dog%d' % l, inp['diff_out_gain'][l][:, None])
        put('hog%d' % l, inp['hgrn_out_gain'][l][:, None])
        put('glab%d' % l, inp['gla_b_gate'][l].reshape(4, 128).T)
        put('glag%d' % l, inp['gla_out_gain'][l].reshape(2, 128).T)
        put('dl%d' % l, np.broadcast_to(inp['diff_lambda'][l].reshape(1, 256), (128, 256)))
    put('lbl', inp['hgrn_lb_logits'].reshape(2, 8, 128).transpose(2, 0, 1).reshape(128, 16))
    put('relb', np.broadcast_to(inp['rel_bias'].reshape(1, 256), (128, 256)))
    return sc


class Prog:
    def __init__(self, T, debug=False, stop_after=None):
        self.T = T
        self.NT = T // TT
        self.debug = debug
        self.stop_after = stop_after
        self.k = KB()
        self.nc = self.k.nc

    def dbg(self, name, ap, buf, shape, dt):
        if not self.debug:
            return
        self.k.uid += 1
        if name in getattr(self, '_dbgseen', set()):
            name = "%s_%d" % (name, self.k.uid)
        self._dbgseen = getattr(self, '_dbgseen', set()) | {name}
        d = self.k.dram("dbg_" + name, list(shape), dt, kind="ExternalOutput")
        self.k.store('sp', d, ap, buf)

    def col(self, name, j=0, n=1):
        o, _ = SC[name]
        return self.sc[:, o + j:o + j + n]

    def build(self):
        k, nc, T = self.k, self.nc, self.T
        ext = "ExternalInput"
        self.xT = nc.dram_tensor("xT", [D, T], F32, kind=ext).ap()
        self.scd = nc.dram_tensor("smallcols", [128, NSC], F32, kind=ext).ap()
        self.cstd = nc.dram_tensor("cst", [128, 1280], F32, kind=ext).ap()
        self.ohtd = nc.dram_tensor("oht", [2, 128, 33 * 128], F32, kind=ext).ap()
        self.w_ada = nc.dram_tensor("w_ada", [DEPTH, D, 9 * D], F32, kind=ext).ap()
        self.wg = nc.dram_tensor("ffn_w_gate", [DEPTH, 2, D, DFF], F32, kind=ext).ap()
        self.wu = nc.dram_tensor("ffn_w_up", [DEPTH, 2, D, DFF], F32, kind=ext).ap()
        self.wd = nc.dram_tensor("ffn_w_down", [DEPTH, 2, DFF, D], F32, kind=ext).ap()
        self.w_in = nc.dram_tensor("w_in", [DEPTH, D, DIN], F32, kind=ext).ap()
        self.gwu = nc.dram_tensor("gla_w_gate_up", [DEPTH, 16, 512], F32, kind=ext).ap()
        self.w_br = nc.dram_tensor("w_branch", [DEPTH, 3, MIX, D], F32, kind=ext).ap()
        self.w_out = nc.dram_tensor("w_out", [DEPTH, D, D], F32, kind=ext).ap()
        self.outT = nc.dram_tensor("outT", [D, T], F32, kind="ExternalOutput").ap()
        sk = "ExternalOutput" if self.debug else "Internal"
        self.x = k.dram("s_x", [D, T], F32, kind=sk)
        self.h = k.dram("s_h", [D, T], BF16, kind=sk)
        self.aq = k.dram("s_aq", [MIX, T], BF16, kind=sk)
        self.ak = k.dram("s_ak", [MIX, T], BF16, kind=sk)
        self.vtok = [k.dram("s_v%d" % i, [T, MIX], BF16, kind=sk) for i in range(3)]
        self.bq = k.dram("s_bq", [MIX, T], BF16, kind=sk)
        self.bz = k.dram("s_bz", [MIX, T], F32, kind=sk)
        self.bsg = k.dram("s_bsg", [MIX, T], BF16, kind=sk)
        self.cq = k.dram("s_cq", [512, T], BF16, kind=sk)
        self.ck = k.dram("s_ck", [512, T], BF16, kind=sk)
        self.crs = k.dram("s_crs", [MIX, T], BF16, kind=sk)
        self.cgd = k.dram("s_cgd", [16, T], BF16, kind=sk)
        self.gat = k.dram("s_gat", [3 * D, T], BF16, kind=sk)
        self.y = [k.dram("s_y%d" % i, [MIX, T], BF16, kind=sk) for i in range(3)]
        self.dbg_mod = k.dram("s_mod", [128, 2 * 144], F32, kind=sk)

        self.pb = []
        for i in range(8):
            t = k.gstack.enter_context(nc.psum_tensor("pb%d" % i, [128, 512], F32))
            self.pb.append((t, Buf("pb%d" % i)))

        self.setup()
        stages = []
        for l in range(DEPTH):
            stages += [('norm', l, 0), ('ffn', l, 0), ('norm', l, 1), ('mixin', l), ('attn', l),
                       ('gla', l, 'B'), ('gla', l, 'C'), ('merge', l), ('norm', l, 2), ('ffn', l, 1),
                       ('fnorm', l)]
        for st in stages:
            getattr(self, 'ph_' + st[0])(*st[1:])
            if self.stop_after is not None and tuple(st) == tuple(self.stop_after):
                break
        return k.emit()

    def setup(self):
        k, nc = self.k, self.nc
        self.sc, self.sc_b = k.sbb("sc", [128, NSC], F32)
        self.cst, self.cst_b = k.sbb("cst", [128, 1280], F32)
        self.ones32, self.ones32_b = k.sbb("ones32", [128, 128], F32)
        self.ones16, self.ones16_b = k.sbb("ones16", [128, 128], BF16)
        self.ident16, self.ident16_b = k.sbb("ident16", [128, 128], BF16)
        self.biasT, self.biasT_b = k.sbb("biasT", [128, 8, 2, 128], F32)
        self.mod, self.mod_b = k.sbb("mod", [128, 2 * 144], F32)
        self.colsA, self.colsA_b = k.sbb("colsA", [128, 2 * 3 * 16], F32)
        self.colsG, self.colsG_b = k.sbb("colsG", [128, 2 * 3 * 16], F32)
        self.misc, self.misc_b = k.sbb("misc", [128, 64], F32)
        self.wup16, self.wup16_b = k.sbb("wup16", [16, 2 * 512], BF16)
        self.x_cur = self.xT

        with k.phase():
            k.load('sp', self.sc[:, :], self.sc_b, self.scd)
            k.load('sp', self.cst[:, :], self.cst_b, self.cstd)
            k.op('dve', lambda e: e.memset(self.ones32[:, :], 1.0), writes=[self.ones32_b])
            k.op('dve', lambda e: e.memset(self.ones16[:, :], 1.0), writes=[self.ones16_b])
            k.op('dve', lambda e: e.tensor_copy(self.ident16[:, :], self.cst[:, 0:128]),
                 reads=[self.cst_b], writes=[self.ident16_b])
            for l in range(DEPTH):
                k.dma('pool', self.wup16[:, l * 512:(l + 1) * 512], self.gwu[l], writes=[self.wup16_b])
            oh, oh_b = k.sbb("oh", [128, 33 * 128], F32)
            ro, _ = SC['relb']
            for typ in range(2):
                k.load('sp', oh[:, :], oh_b, self.ohtd[typ])
                for h in range(8):
                    dst = self.biasT[:, h, typ, :]
                    k.op('dve', lambda e, dst=dst: e.tensor_copy(dst, oh[:, 32 * 128:33 * 128]),
                         reads=[oh_b], writes=[self.biasT_b])
                    for b in range(32):
                        sc_ap = self.sc[:, ro + b * 8 + h:ro + b * 8 + h + 1]
                        k.op('dve', lambda e, dst=dst, b=b, sc_ap=sc_ap: e.scalar_tensor_tensor(
                            out=dst, in0=oh[:, b * 128:(b + 1) * 128], scalar=sc_ap, in1=dst,
                            op0=ALU.mult, op1=ALU.add),
                            reads=[oh_b, self.sc_b], writes=[self.biasT_b])
            k.strict = True
            tmp, tmp_b = k.sbb("stmp", [128, 64], F32)
            red, red_b = k.sbb("sred", [128, 4], F32)
            junk, junk_b = k.sbb("sjunk", [128, 64], F32)
            red2, red2_b = k.sbb("sred2", [128, 2], F32)
            for l in range(DEPTH):
                lam_init = 0.8 - 0.6 * math.exp(-0.3 * l)
                o, _ = SC['dl%d' % l]
                for pair in range(2):
                    a0 = self.sc[:, o + pair * 128:o + pair * 128 + 64]
                    a1 = self.sc[:, o + pair * 128 + 64:o + pair * 128 + 128]
                    k.op('dve', lambda e, a0=a0, a1=a1: e.tensor_tensor(tmp[:, :], a0, a1, ALU.mult),
                         reads=[self.sc_b], writes=[tmp_b])
                    k.op('dve', lambda e, pair=pair: e.tensor_tensor_scan(
                        out=junk[:, :], data0=self.ones32[:, 0:64], data1=tmp[:, :], initial=0.0,
                        op0=ALU.mult, op1=ALU.add), reads=[tmp_b, self.ones32_b], writes=[junk_b])
                    k.op('dve', lambda e, pair=pair: e.tensor_copy(red[:, pair:pair + 1], junk[:, 63:64]),
                         reads=[junk_b], writes=[red_b])
                for pair in range(2):
                    k.op('act', lambda e, pair=pair: e.activation(out=red2[:, pair:pair + 1], in_=red[:, pair:pair + 1],
                                                                   func=AF.Exp), reads=[red_b], writes=[red2_b])
                k.op('dve', lambda e, l=l, lam_init=lam_init: e.tensor_tensor(
                    self.misc[:, l * 16:l * 16 + 1], red2[:, 1:2], red2[:, 0:1], ALU.subtract),
                    reads=[red2_b], writes=[self.misc_b])
                k.op('dve', lambda e, l=l, lam_init=lam_init: e.tensor_scalar(
                    self.misc[:, l * 16:l * 16 + 1], self.misc[:, l * 16:l * 16 + 1], -lam_init, None,
                    op0=ALU.add), reads=[self.misc_b], writes=[self.misc_b])
                k.op('dve', lambda e, l=l, lam_init=lam_init: e.tensor_scalar(
                    self.misc[:, l * 16 + 1:l * 16 + 2], self.col('dog%d' % l), 1.0 - lam_init, None,
                    op0=ALU.mult), reads=[self.sc_b], writes=[self.misc_b])
            lo, _ = SC['lbl']
            k.op('dve', lambda e: e.tensor_tensor(tmp[:, 0:8], self.sc[:, lo + 8:lo + 16],
                                                  self.sc[:, lo:lo + 8], ALU.subtract),
                 reads=[self.sc_b], writes=[tmp_b])
            k.op('act', lambda e: e.activation(out=self.misc[:, 32:40], in_=tmp[:, 0:8], func=AF.Sigmoid),
                 reads=[tmp_b], writes=[self.misc_b])
            k.op('dve', lambda e: e.tensor_scalar(self.misc[:, 40:48], self.misc[:, 32:40], -1.0, 1.0,
                                                  op0=ALU.mult, op1=ALU.add),
                 reads=[self.misc_b], writes=[self.misc_b])
            k.op('dve', lambda e: e.memset(self.misc[:, 48:56], 0.0), writes=[self.misc_b])
            k.op('dve', lambda e: e.memset(self.misc[:, 56:64], 1.0), writes=[self.misc_b])

        with k.phase():
            cond2, cond2_b = k.sbb("cond2", [128, 16, 2], F32)
            co, _ = SC['c']
            for j in range(2):
                k.op('act', lambda e, j=j: e.activation(out=cond2[:, :, j], in_=self.sc[:, co:co + 16],
                                                         func=AF.Silu),
                     reads=[self.sc_b], writes=[cond2_b])
            wrot = k.rot("wada", 2, [128, 16, 512], F32)
            for l in range(DEPTH):
                ps, ps_b = self.pb[l]
                wv = self.w_ada[l].rearrange("(kc p) n -> p kc n", p=128)
                for s in range(36):
                    wt, wt_b = wrot.next()
                    k.load('sp', wt[:, :, :], wt_b, wv[:, :, s * 512:(s + 1) * 512])
                    for n in range(4):
                        cidx = s * 4 + n

                        def grp(e, wt=wt, n=n, cidx=cidx, ps=ps):
                            ins = None
                            for kc in range(KC):
                                ins = e.matmul(ps[:, 2 * cidx:2 * cidx + 2],
                                               lhsT=wt[:, kc, n * 128:(n + 1) * 128],
                                               rhs=cond2[:, kc, :], start=(kc == 0), stop=(kc == KC - 1))
                            return ins
                        k.op('pe', grp, reads=[wt_b, cond2_b], writes=[ps_b])
                bo, _ = SC['bada%d' % l]
                psv = ps[:, 0:288].rearrange("p (c t) -> p c t", t=2)[:, :, 0]
                k.op('dve', lambda e, l=l, psv=psv, bo=bo: e.tensor_tensor(
                    self.mod[:, l * 144:(l + 1) * 144], psv, self.sc[:, bo:bo + 144], ALU.add),
                    reads=[ps_b, self.sc_b], writes=[self.mod_b])
                for i in range(3):
                    base = l * 144 + i * 48
                    go, _ = SC['ng%d_%d' % (l, i)]
                    ao = (l * 3 + i) * 16
                    k.op('dve', lambda e, base=base, go=go, ao=ao: e.scalar_tensor_tensor(
                        out=self.colsA[:, ao:ao + 16], in0=self.mod[:, base + 16:base + 32], scalar=1.0,
                        in1=self.sc[:, go:go + 16], op0=ALU.add, op1=ALU.mult),
                        reads=[self.mod_b, self.sc_b], writes=[self.colsA_b])
                    gs = 1.0 if i == 1 else 0.5
                    k.op('dve', lambda e, base=base, ao=ao, gs=gs: e.tensor_scalar(
                        self.colsG[:, ao:ao + 16], self.mod[:, base + 32:base + 48], gs, None, op0=ALU.mult),
                        reads=[self.mod_b], writes=[self.colsG_b])
            if self.debug:
                k.store('sp', self.dbg_mod, self.mod[:, :], self.mod_b)
                self.dbg("misc", self.misc[:, :], self.misc_b, [128, 64], F32)
                self.dbg("sc", self.sc[:, :], self.sc_b, [128, NSC], F32)
                self.dbg("biasT", self.biasT[:, :, :, :], self.biasT_b, [128, 8, 2, 128], F32)

    def _norm(self, src, dst, out_dt, a_ap, b_ap):
        k = self.k
        srcv = src.rearrange("(kc p) t -> p kc t", p=128)
        dstv = dst.rearrange("(kc p) t -> p kc t", p=128)
        with k.phase():
            xrot = k.rot("nx", 3, [128, KC, TT], F32)
            orot = k.rot("no", 2, [128, KC, TT], out_dt)
            sqrot = k.rot("nsq", 3, [128, TT], F32)
            trot = k.rot("ntmp", 3, [128, TT], F32)
            stdrot = k.rot("nstd", 2, [128, TT], F32)
            rstdrot = k.rot("nrstd", 2, [128, TT], F32)

            def stats_a(tt):
                ts = slice(tt * TT, (tt + 1) * TT)
                xt, xt_b = xrot.next()
                k.load('sp', xt[:, :, :], xt_b, srcv[:, :, ts])
                ps, ps_b = self.pb[tt % 2]
                for kc in range(KC):
                    sq, sq_b = sqrot.next()
                    k.op('act', lambda e, sq=sq, kc=kc: e.activation(out=sq[:, :], in_=xt[:, kc, :], func=AF.Square),
                         reads=[xt_b], writes=[sq_b])
                    k.op('pe', lambda e, sq=sq, kc=kc: e.matmul(ps[:, :], lhsT=self.ones32[:, :], rhs=sq[:, :],
                                                                start=(kc == 0), stop=(kc == KC - 1)),
                         reads=[sq_b, self.ones32_b], writes=[ps_b])
                return dict(ts=ts, xt=xt, xt_b=xt_b, ps=ps, ps_b=ps_b)

            def stats_b(cx):
                ps, ps_b = cx['ps'], cx['ps_b']
                std, std_b = stdrot.next()
                rstd, rstd_b = rstdrot.next()
                k.op('act', lambda e: e.activation(out=std[:, :], in_=ps[:, :], func=AF.Ln,
                                                   bias=self.epsc[:, 0:1], scale=1.0 / D),
                     reads=[ps_b], writes=[std_b])
                k.op('act', lambda e: e.activation(out=rstd[:, :], in_=std[:, :], func=AF.Exp, scale=-0.5),
                     reads=[std_b], writes=[rstd_b])
                cx['rstd'], cx['rstd_b'] = rstd, rstd_b

            def normalize(cx):
                xt, xt_b, rstd, rstd_b, ts = cx['xt'], cx['xt_b'], cx['rstd'], cx['rstd_b'], cx['ts']
                ot, ot_b = orot.next()
                for kc in range(KC):
                    if b_ap is None:
                        k.op('dve', lambda e, kc=kc: e.scalar_tensor_tensor(
                            out=ot[:, kc, :], in0=xt[:, kc, :], scalar=a_ap[:, kc:kc + 1], in1=rstd[:, :],
                            op0=ALU.mult, op1=ALU.mult), reads=[xt_b, rstd_b], writes=[ot_b])
                    else:
                        tp, tp_b = trot.next()
                        k.op('dve', lambda e, tp=tp, kc=kc: e.scalar_tensor_tensor(
                            out=tp[:, :], in0=xt[:, kc, :], scalar=a_ap[:, kc:kc + 1], in1=rstd[:, :],
                            op0=ALU.mult, op1=ALU.mult), reads=[xt_b, rstd_b], writes=[tp_b])
                        k.op('act', lambda e, tp=tp, kc=kc: e.activation(
                            out=ot[:, kc, :], in_=tp[:, :], func=AF.Identity, bias=b_ap[:, kc:kc + 1]),
                            reads=[tp_b], writes=[ot_b])
                k.store('sp', dstv[:, :, ts], ot[:, :, :], ot_b)

            prev = None
            for tt in range(self.NT):
                cx = stats_a(tt)
                if prev is not None:
                    normalize(prev)
                stats_b(cx)
                prev = cx
            normalize(prev)

    def ph_norm(self, l, i):
        ao = (l * 3 + i) * 16
        base = l * 144 + i * 48
        self._ensure_eps()
        self._norm(self.x_cur, self.h, BF16, self.colsA[:, ao:ao + 16], self.mod[:, base:base + 16])

    def ph_fnorm(self, l):
        go, _ = SC['ng%d_3' % l]
        self._ensure_eps()
        dst = self.outT if l == DEPTH - 1 else self.x
        self._norm(self.x_cur, dst, F32, self.sc[:, go:go + 16], None)
        self.x_cur = dst

    def _ensure_eps(self):
        if not hasattr(self, 'epsc'):
            k = self.k
            self.epsc, self.epsc_b = k.sbb("epsc", [128, 1], F32)
            k.op('dve', lambda e: e.memset(self.epsc[:, :], EPS), writes=[self.epsc_b])

    def _gemm(self, wsrcs, kcn, col0, ncols, rhs_list, wrots, psrot, evac, gw=256):
        k = self.k
        c = 0
        j = 0
        while c < ncols:
            g = min(gw, ncols - c)
            slabs = []
            for wsrc, wrot in zip(wsrcs, wrots):
                wt, wt_b = wrot.next()
                k.dma('pool', wt[:, :, 0:g], wsrc[:, :, col0 + c:col0 + c + g], writes=[wt_b])
                slabs.append((wt, wt_b))
            cc = 0
            while cc < g:
                m = min(128, g - cc)
                for si, (rhs, rhs_b) in enumerate(rhs_list):
                    pss = []
                    for (wt, wt_b) in slabs:
                        ps, ps_b = psrot.next()

                        def grp(e, wt=wt, ps=ps, cc=cc, m=m, rhs=rhs):
                            ins = None
                            for kc in range(kcn):
                                ins = e.matmul(ps[0:m, :], lhsT=wt[:, kc, cc:cc + m], rhs=rhs[:, kc, :],
                                               start=(kc == 0), stop=(kc == kcn - 1))
                            return ins
                        k.op('pe', grp, reads=[wt_b, rhs_b], writes=[ps_b])
                        pss.append((ps, ps_b))
                    evac(j, c + cc, m, pss, si)
                j += 1
                cc += m
            c += g

    def ph_ffn(self, l, w):
        k = self.k
        i = 0 if w == 0 else 2
        go = (l * 3 + i) * 16
        hv = self.h.rearrange("(kc p) t -> p kc t", p=128)
        xsrc = self.x_cur.rearrange("(kc p) t -> p kc t", p=128)
        xdst = self.x.rearrange("(kc p) t -> p kc t", p=128)
        wgv = self.wg[l, w].rearrange("(kc p) n -> p kc n", p=128)
        wuv = self.wu[l, w].rearrange("(kc p) n -> p kc n", p=128)
        wdv = self.wd[l, w].rearrange("(kc p) n -> p kc n", p=128)
        with k.phase():
            hts = [k.sbb("fh%d" % i_, [128, KC, TT], BF16) for i_ in range(2)]
            acts = [k.sbb("fact%d" % i_, [128, FC, TT], BF16) for i_ in range(2)]
            wgrot = k.rot("fwg", 2, [128, KC, 128], BF16)
            wurot = k.rot("fwu", 2, [128, KC, 128], BF16)
            wdrot = k.rot("fwd", 2, [128, FC, 128], BF16)
            sgrot = k.rot("fsg", 2, [128, TT], F32)
            xorot = k.rot("fxo", 3, [128, TT], F32)
            xnrot = k.rot("fxn", 3, [128, TT], F32)
            psrot = Rot(self.pb[0:6])
            psrot2 = Rot(self.pb[6:8])
            for t2 in range(0, self.NT, 2):
                tts = list(range(t2, min(t2 + 2, self.NT)))
                subs = []
                tsl = []
                for si, tt in enumerate(tts):
                    ht, ht_b = hts[si]
                    k.load('sp', ht[:, :, :], ht_b, hv[:, :, tt * TT:(tt + 1) * TT])
                    subs.append((ht, ht_b))
                    tsl.append(slice(tt * TT, (tt + 1) * TT))

                def evac_gu(j, c, m, pss, si):
                    (pg, pg_b), (pu, pu_b) = pss
                    act, act_b = acts[si]
                    sg, sg_b = sgrot.next()
                    k.op('act', lambda e: e.activation(out=sg[:, :], in_=pg[:, :], func=AF.Silu),
                         reads=[pg_b], writes=[sg_b])
                    k.op('dve', lambda e: e.tensor_tensor(act[:, j, :], sg[:, :], pu[:, :], ALU.mult),
                         reads=[sg_b, pu_b], writes=[act_b])
                self._gemm([wgv, wuv], KC, 0, DFF, subs, [wgrot, wurot], psrot, evac_gu, gw=128)

                def evac_d(j, c, m, pss, si):
                    (po, po_b), = pss
                    xo, xo_b = xorot.next()
                    k.load('sp', xo[:, :], xo_b, xsrc[:, j, tsl[si]])
                    xn, xn_b = xnrot.next()
                    k.op('dve', lambda e: e.scalar_tensor_tensor(
                        out=xn[:, :], in0=po[:, :], scalar=self.colsG[:, go + j:go + j + 1], in1=xo[:, :],
                        op0=ALU.mult, op1=ALU.add), reads=[po_b, xo_b, self.colsG_b], writes=[xn_b])
                    k.store('sp', xdst[:, j, tsl[si]], xn[:, :], xn_b)
                self._gemm([wdv], FC, 0, D, [acts[si] for si in range(len(tts))], [wdrot], psrot2, evac_d, gw=128)
        self.x_cur = self.x

    def ph_mixin(self, l):
        k = self.k
        hv = self.h.rearrange("(kc p) t -> p kc t", p=128)
        wv = self.w_in[l].rearrange("(kc p) n -> p kc n", p=128)
        self._ensure_eps()
        with k.phase():
            hrot = k.rot("mh", 4, [128, KC, TT], BF16)
            wrot = k.rot("mw", 2, [128, KC, 256], BF16)
            wtrot = k.rot("mwt", 2, [128, KC, 512], BF16)
            o16rot = k.rot("mo16", 6, [128, TT], BF16)
            o32rot = k.rot("mo32", 3, [128, TT], F32)
            sqrot = k.rot("msq", 2, [128, TT], F32)
            strot = k.rot("mst", 2, [128, TT], F32)
            rsrot = k.rot("mrs", 2, [128, TT], F32)
            psrot = Rot(self.pb[0:5])
            psrot2 = Rot(self.pb[5:8])
            for t2 in range(0, self.NT, 2):
                tts = list(range(t2, min(t2 + 2, self.NT)))
                subs = []
                tsl = []
                for tt in tts:
                    ht, ht_b = hrot.next()
                    k.load('sp', ht[:, :, :], ht_b, hv[:, :, tt * TT:(tt + 1) * TT])
                    subs.append((ht, ht_b))
                    tsl.append(slice(tt * TT, (tt + 1) * TT))

                def ev_qknorm(dst, gcol):
                    def ev(j, c, m, pss, si):
                        (ps, ps_b), = pss
                        sq, sq_b = sqrot.next()
                        k.op('act', lambda e: e.activation(out=sq[:, :], in_=ps[:, :], func=AF.Square),
                             reads=[ps_b], writes=[sq_b])
                        p2, p2_b = psrot2.next()
                        k.op('pe', lambda e: e.matmul(p2[:, :], lhsT=self.cst[:, 128:256], rhs=sq[:, :],
                                                      start=True, stop=True),
                             reads=[sq_b, self.cst_b], writes=[p2_b])
                        st, st_b = strot.next()
                        k.op('act', lambda e: e.activation(out=st[:, :], in_=p2[:, :], func=AF.Ln,
                                                           bias=self.epsc[:, 0:1], scale=1.0 / 64),
                             reads=[p2_b], writes=[st_b])
                        rs, rs_b = rsrot.next()
                        k.op('act', lambda e: e.activation(out=rs[:, :], in_=st[:, :], func=AF.Exp, scale=-0.5), reads=[st_b], writes=[rs_b])
                        o, o_b = o16rot.next()
                        k.op('dve', lambda e: e.scalar_tensor_tensor(
                            out=o[:, :], in0=ps[:, :], scalar=gcol, in1=rs[:, :], op0=ALU.mult, op1=ALU.mult),
                            reads=[ps_b, rs_b, self.sc_b], writes=[o_b])
                        k.store('sp', dst[c:c + m, tsl[si]], o[0:m, :], o_b)
                    return ev

                def ev_act(dst, func):
                    def ev(j, c, m, pss, si):
                        (ps, ps_b), = pss
                        o, o_b = o16rot.next()
                        k.op('act', lambda e: e.activation(out=o[0:m, :], in_=ps[0:m, :], func=func),
                             reads=[ps_b], writes=[o_b])
                        k.store('sp', dst[c:c + m, tsl[si]], o[0:m, :], o_b)
                    return ev

                def ev_copy16(dst):
                    def ev(j, c, m, pss, si):
                        (ps, ps_b), = pss
                        o, o_b = o16rot.next()
                        k.op('dve', lambda e: e.tensor_copy(o[0:m, :], ps[0:m, :]), reads=[ps_b], writes=[o_b])
                        k.store('sp', dst[c:c + m, tsl[si]], o[0:m, :], o_b)
                    return ev

                def ev_copy32(dst):
                    def ev(j, c, m, pss, si):
                        (ps, ps_b), = pss
                        o, o_b = o32rot.next()
                        k.op('dve', lambda e: e.tensor_copy(o[0:m, :], ps[0:m, :]), reads=[ps_b], writes=[o_b])
                        k.store('sp', dst[c:c + m, tsl[si]], o[0:m, :], o_b)
                    return ev

                fm = [
                    (O_AQ, 1024, ev_qknorm(self.aq, self.col('qg%d' % l))),
                    (O_AK, 1024, ev_qknorm(self.ak, self.col('kg%d' % l))),
                    (O_BQ, 1024, ev_copy16(self.bq)),
                    (O_BF, 1024, ev_copy32(self.bz)),
                    (O_BG, 1024, ev_act(self.bsg, AF.Sigmoid)),
                    (O_CQ, 512, ev_copy16(self.cq)),
                    (O_CK, 512, ev_copy16(self.ck)),
                    (O_CR, 1024, ev_act(self.crs, AF.Silu)),
                    (O_CGD, 16, ev_copy16(self.cgd)),
                    (O_GATE, 3 * D, ev_act(self.gat, AF.Sigmoid)),
                ]
                for col0, ncols, ev in fm:
                    self._gemm([wv], KC, col0, ncols, subs, [wrot], psrot, ev)
                for vi, col0 in enumerate((O_AV, O_BI, O_CV)):
                    for g in range(2):
                        wt, wt_b = wtrot.next()
                        k.dma('pool', wt[:, :, :], wv[:, :, col0 + g * 512:col0 + (g + 1) * 512], writes=[wt_b])
                        for si, (ht, ht_b) in enumerate(subs):
                            for tb in range(4):
                                ps, ps_b = psrot.next()

                                def grp(e, wt=wt, ps=ps, tb=tb, ht=ht):
                                    ins = None
                                    for kc in range(KC):
                                        ins = e.matmul(ps[:, :], lhsT=ht[:, kc, tb * 128:(tb + 1) * 128],
                                                       rhs=wt[:, kc, :], start=(kc == 0), stop=(kc == KC - 1))
                                    return ins
                                k.op('pe', grp, reads=[wt_b, ht_b], writes=[ps_b])
                                o, o_b = o16rot.next()
                                if tb % 2 == 0:
                                    k.op('act', lambda e, o=o, ps=ps: e.activation(out=o[:, :], in_=ps[:, :],
                                                                                    func=AF.Identity),
                                         reads=[ps_b], writes=[o_b])
                                else:
                                    k.op('dve', lambda e, o=o, ps=ps: e.tensor_copy(o[:, :], ps[:, :]),
                                         reads=[ps_b], writes=[o_b])
                                r0 = tts[si] * TT + tb * 128
                                k.store('sp', self.vtok[vi][r0:r0 + 128, g * 512:(g + 1) * 512], o[:, :], o_b)

    def ph_attn(self, l):
        k = self.k
        T, NT = self.T, self.NT
        NKB = T // 128
        SCALE = 0.125
        ro, _ = SC['relb']
        self._ensure_eps()
        with k.phase():
            ktrot = k.rot("akT", 2, [128, T], BF16)
            vrot = k.rot("av", 2, [128, NKB, 128], BF16)
            qrot = k.rot("aq", 2, [128, TT], BF16)
            erot = [k.rot("ae%d" % m, 3, [128, TT], BF16) for m in range(2)]
            tmprot = k.rot("atmp", 4, [128, 128], F32)
            rrot = k.rot("ar", 2, [128, TT], F32)
            trot = k.rot("at", 2, [128, TT], F32)
            d_, d_b = k.sbb("ad", [128, TT], F32)
            sq, sq_b = k.sbb("asq", [128, TT], F32)
            st, st_b = k.sbb("ast", [128, TT], F32)
            rs, rs_b = k.sbb("ars", [128, TT], F32)
            yrot = k.rot("ay", 2, [128, TT], BF16)
            srot = Rot([(self.pb[0], self.pb[1]), (self.pb[2], self.pb[3])])
            (po0, po0_b), (po1, po1_b), (pz0, pz0_b), (pz1, pz1_b) = self.pb[4:8]
            pos = [(po0, po0_b), (po1, po1_b)]
            pzs = [(pz0, pz0_b), (pz1, pz1_b)]
            neg_lam = self.misc[:, l * 16:l * 16 + 1]
            gcol = self.misc[:, l * 16 + 1:l * 16 + 2]
            drot = k.rot("adr", 2, [128, TT], F32)
            pending_tail = []

            def post_tail(dt_, dt_b, h, Q):
                k.op('act', lambda e: e.activation(out=sq[:, :], in_=dt_[:, :], func=AF.Square),
                     reads=[dt_b], writes=[sq_b])
                (pn, pn_b) = srot.next()[0]
                k.op('pe', lambda e: e.matmul(pn[:, :], lhsT=self.ones32[:, :], rhs=sq[:, :], start=True, stop=True),
                     reads=[sq_b, self.ones32_b], writes=[pn_b])
                k.op('act', lambda e: e.activation(out=st[:, :], in_=pn[:, :], func=AF.Ln,
                                                   bias=self.epsc[:, 0:1], scale=1.0 / 128),
                     reads=[pn_b], writes=[st_b])
                k.op('act', lambda e: e.activation(out=rs[:, :], in_=st[:, :], func=AF.Exp, scale=-0.5),
                     reads=[st_b], writes=[rs_b])
                yt, yt_b = yrot.next()
                k.op('dve', lambda e: e.scalar_tensor_tensor(
                    out=yt[:, :], in0=dt_[:, :], scalar=gcol, in1=rs[:, :], op0=ALU.mult, op1=ALU.mult),
                    reads=[dt_b, rs_b, self.misc_b], writes=[yt_b])
                k.store('sp', self.y[0][h * 128:(h + 1) * 128, Q * TT:(Q + 1) * TT], yt[:, :], yt_b)

            for h in range(8):
                kt, kt_b = ktrot.next()
                k.load('sp', kt[:, :], kt_b, self.ak[h * 128:(h + 1) * 128, :])
                vt, vt_b = vrot.next()
                vsrc_h = self.vtok[0][:, h * 128:(h + 1) * 128].rearrange("(b p) d -> p b d", p=128)
                for b0 in range(0, NKB, 8):
                    b1 = min(NKB, b0 + 8)
                    k.dma('sp', vt[:, b0:b1, :], vsrc_h[:, b0:b1, :], writes=[vt_b])
                cb = self.sc[:, ro + 15 * 8 + h:ro + 15 * 8 + h + 1]
                for Q in range(NT):
                    qt, qt_b = qrot.next()
                    k.load('sp', qt[:, :], qt_b, self.aq[h * 128:(h + 1) * 128, Q * TT:(Q + 1) * TT])
                    nkb = 4 * Q + 4

                    def emit_S(kb):
                        c0_ = max(0, kb - 4 * Q) * 128
                        (s0, s1) = srot.next()
                        pair = [s0, s1]
                        for m in range(2):
                            ps, ps_b = pair[m]
                            k.op('pe', lambda e, ps=ps, m=m, kb=kb, c0_=c0_, kt=kt, qt=qt: e.matmul(
                                ps[:, c0_:TT], lhsT=kt[m * 64:(m + 1) * 64, kb * 128:(kb + 1) * 128],
                                rhs=qt[m * 64:(m + 1) * 64, c0_:TT], start=True, stop=True),
                                reads=[kt_b, qt_b], writes=[ps_b])
                        return pair
                    nxt = emit_S(0)
                    for kb in range(nkb):
                        i0 = max(0, kb - 4 * Q)
                        c0 = i0 * 128
                        sp_ = nxt
                        if kb + 1 < nkb:
                            nxt = emit_S(kb + 1)
                        es = []
                        for m in range(2):
                            ps, ps_b = sp_[m]
                            et, et_b = erot[m].next()
                            cplain = c0
                            for typ, i in ((0, kb - 4 * Q), (1, kb - 4 * Q + 1)):
                                if 0 <= i <= 3:
                                    tp, tp_b = tmprot.next()
                                    k.op('dve', lambda e, tp=tp, ps=ps, i=i, typ=typ, h=h: e.scalar_tensor_tensor(
                                        out=tp[:, :], in0=ps[:, i * 128:(i + 1) * 128], scalar=SCALE,
                                        in1=self.biasT[:, h, typ, :], op0=ALU.mult, op1=ALU.add),
                                        reads=[ps_b, self.biasT_b], writes=[tp_b])
                                    k.op('act', lambda e, tp=tp, et=et, i=i: e.activation(
                                        out=et[:, i * 128:(i + 1) * 128], in_=tp[:, :], func=AF.Exp),
                                        reads=[tp_b], writes=[et_b])
                                    cplain = max(cplain, (i + 1) * 128)
                            if cplain < TT:
                                k.op('act', lambda e, et=et, ps=ps, cplain=cplain, cb=cb: e.activation(
                                    out=et[:, cplain:TT], in_=ps[:, cplain:TT], func=AF.Exp, bias=cb, scale=SCALE),
                                    reads=[ps_b, self.sc_b], writes=[et_b])
                            es.append((et, et_b))
                        for m in range(2):
                            et, et_b = es[m]
                            po, po_b = pos[m]
                            pz, pz_b = pzs[m]
                            k.op('pe', lambda e, po=po, et=et, kb=kb, c0=c0, vt=vt, nkb=nkb: e.matmul(
                                po[:, c0:TT], lhsT=vt[:, kb, :], rhs=et[:, c0:TT], start=(kb == 0),
                                stop=(kb == nkb - 1)), reads=[vt_b, et_b], writes=[po_b])
                            k.op('pe', lambda e, pz=pz, et=et, kb=kb, c0=c0, nkb=nkb: e.matmul(
                                pz[:, c0:TT], lhsT=self.ones16[:, :], rhs=et[:, c0:TT], start=(kb == 0),
                                stop=(kb == nkb - 1)), reads=[self.ones16_b, et_b], writes=[pz_b])
                    ts_ = []
                    for m in range(2):
                        r, r_b = rrot.next()
                        k.op('dve', lambda e, r=r, m=m: e.reciprocal(r[:, :], pzs[m][0][:, :]),
                             reads=[pzs[m][1]], writes=[r_b])
                        t_, t_b = trot.next()
                        k.op('dve', lambda e, t_=t_, r=r, m=m: e.tensor_tensor(t_[:, :], pos[m][0][:, :], r[:, :],
                                                                             ALU.mult),
                             reads=[pos[m][1], r_b], writes=[t_b])
                        ts_.append((t_, t_b))
                    dt_, dt_b = drot.next()
                    k.op('dve', lambda e, a=ts_[0][0], b=ts_[1][0], dt_=dt_: e.scalar_tensor_tensor(
                        out=dt_[:, :], in0=b[:, :], scalar=neg_lam, in1=a[:, :], op0=ALU.mult, op1=ALU.add),
                        reads=[ts_[0][1], ts_[1][1], self.misc_b], writes=[dt_b])
                    if pending_tail:
                        post_tail(*pending_tail.pop(0))
                    pending_tail.append((dt_, dt_b, h, Q))
            while pending_tail:
                post_tail(*pending_tail.pop(0))

    def ph_gla(self, l, kind):
        k = self.k
        NT = self.NT
        isB = (kind == 'B')
        H = 8 if isB else 4
        DV = 128 if isB else 256
        NV = DV // 128
        vsrc = self.vtok[1] if isB else self.vtok[2]
        ydst = self.y[1] if isB else self.y[2]
        self._ensure_eps()
        with k.phase():
            S32 = [[k.sbb("gS32_%d_%d" % (h, i), [128, DV], F32) for i in range(2)] for h in range(H)]
            for h in range(H):
                k.op('dve', lambda e, h=h: e.memset(S32[h][0][0][:, :], 0.0), writes=[S32[h][0][1]])
            vrot = k.rot("gv", 2, [32, 16, MIX], BF16)
            cgrot = k.rot("gcg", 2, [16, TT], BF16)
            qrot = k.rot("gq", 2, [128, TT], BF16)
            kkrot = k.rot("gkk", 2, [128, TT], F32 if isB else BF16)
            zrot = k.rot("gz", 2, [128, TT], F32)
            gaterot = k.rot("ggate", 2, [128, NV, TT], BF16)
            f_, f_b = k.sbb("gf", [128, TT], F32)
            g_, g_b = k.sbb("gg", [128, TT], F32)
            b_, b_b = k.sbb("gb", [128, TT], F32)
            nb, nb_b = k.sbb("gnb", [128, TT], F32)
            ek, ek_b = k.sbb("gek", [128, TT], F32)
            ebrot = k.rot("geb", 2, [128, TT], F32)
            qhrot = k.rot("gqh", 2, [128, TT], BF16)
            ktlrot = k.rot("gkt", 2, [128, TT], BF16)
            PTrot = k.rot("gPT", 2, [32, TT], BF16)
            ktokrot = k.rot("gktok", 2, [32, 16, 128], BF16)
            tdall = [k.sb("gtd%d" % i, [128, 16, DV], F32) for i in range(2)]
            tdall_b = [[Buf("gtd%d_%d" % (i, c)) for c in range(16)] for i in range(2)]
            s16all = [k.sb("gs16_%d" % i, [128, 16, DV], BF16) for i in range(2)]
            s16all_b = [[Buf("gs16_%d_%d" % (i, c)) for c in range(16)] for i in range(2)]
            t1rot = k.rot("gt1", 2 * NV, [128, TT], F32)
            sqrot = k.rot("gsq", 2, [128, TT], F32)
            st, st_b = k.sbb("gst", [128, TT], F32)
            rs, rs_b = k.sbb("grs", [128, TT], F32)
            yt_rot = k.rot("gy", 2, [128, NV, TT], BF16)
            (pss, pss_b) = self.pb[0]
            (ptp, ptp_b) = self.pb[1]
            porot = Rot([(self.pb[2], self.pb[3]), (self.pb[4], self.pb[5])])
            pdrot = Rot([self.pb[6], self.pb[7]])
            rmask = self.cst[:, 256:768]
            cmask = self.cst[0:32, 768:1280]
            qscale = 1.0 if isB else 128.0 ** -0.5
            lbo = 48 if l == 0 else 32
            omo = 56 if l == 0 else 40
            slot = [0]

            def pre(tt, h, vt, vt_b, cg, cg_b, tick):
                def op(*a, **kw):
                    k.op(*a, **kw)
                    tick()
                ts = slice(tt * TT, (tt + 1) * TT)
                hs = slice(h * 128, (h + 1) * 128)
                si = slot[0] % 2
                slot[0] += 1
                qt, qt_b = qrot.next()
                k.load('sp', qt[:, :], qt_b, (self.bq if isB else self.cq)[hs, ts])
                gt, gt_b = gaterot.next()
                if isB:
                    k.load('sp', gt[:, 0, :], gt_b, self.bsg[hs, ts])
                else:
                    k.load('sp', gt[:, :, :], gt_b,
                           self.crs[h * 256:(h + 1) * 256, ts].rearrange("(j p) t -> p j t", p=128))
                kk, kk_b = kkrot.next()
                if isB:
                    zt, zt_b = zrot.next()
                    k.load('sp', zt[:, :], zt_b, self.bz[hs, ts])
                    op('act', lambda e: e.activation(out=f_[:, :], in_=zt[:, :], func=AF.Sigmoid),
                         reads=[zt_b], writes=[f_b])
                    op('dve', lambda e: e.tensor_scalar(
                        f_[:, :], f_[:, :], self.misc[:, omo + h:omo + h + 1], self.misc[:, lbo + h:lbo + h + 1],
                        op0=ALU.mult, op1=ALU.add), reads=[f_b, self.misc_b], writes=[f_b])
                    op('act', lambda e: e.activation(out=g_[:, :], in_=f_[:, :], func=AF.Ln),
                         reads=[f_b], writes=[g_b])
                    op('dve', lambda e: e.tensor_scalar(kk[:, :], f_[:, :], -1.0, 1.0, op0=ALU.mult, op1=ALU.add),
                         reads=[f_b], writes=[kk_b])
                else:
                    (pm, pm_b) = pdrot.next()
                    op('pe', lambda e: e.matmul(
                        pm[:, :], lhsT=self.wup16[:, l * 512 + h * 128:l * 512 + (h + 1) * 128], rhs=cg[:, :],
                        start=True, stop=True), reads=[self.wup16_b, cg_b], writes=[pm_b])
                    op('act', lambda e: e.activation(out=f_[:, :], in_=pm[:, :], func=AF.Sigmoid,
                                                       bias=self.col('glab%d' % l, h)),
                         reads=[pm_b, self.sc_b], writes=[f_b])
                    op('act', lambda e: e.activation(out=g_[:, :], in_=f_[:, :], func=AF.Ln),
                         reads=[f_b], writes=[g_b])
                    op('dve', lambda e: e.tensor_scalar(g_[:, :], g_[:, :], 1.0 / 16.0, None, op0=ALU.mult),
                         reads=[g_b], writes=[g_b])
                    k.load('sp', kk[:, :], kk_b, self.ck[hs, ts])
                eb, eb_b = ebrot.next()
                qh, qh_b = qhrot.next()
                ktl, ktl_b = ktlrot.next()
                PT, PT_b = PTrot.next()
                ktok, ktok_b = ktokrot.next()
                op('dve', lambda e: e.tensor_tensor_scan(out=b_[:, :], data0=rmask, data1=g_[:, :],
                                                           initial=0.0, op0=ALU.mult, op1=ALU.add),
                     reads=[g_b, self.cst_b], writes=[b_b])
                op('act', lambda e: e.activation(out=eb[:, :], in_=b_[:, :], func=AF.Exp),
                     reads=[b_b], writes=[eb_b])
                op('dve', lambda e: e.tensor_scalar(nb[:, :], b_[:, :], -1.0, 80.0, op0=ALU.mult, op1=ALU.min),
                     reads=[b_b], writes=[nb_b])
                op('act', lambda e: e.activation(out=ek[:, :], in_=nb[:, :], func=AF.Exp),
                     reads=[nb_b], writes=[ek_b])
                op('dve', lambda e: e.scalar_tensor_tensor(
                    out=qh[:, :], in0=qt[:, :], scalar=qscale, in1=eb[:, :], op0=ALU.mult, op1=ALU.mult),
                    reads=[qt_b, eb_b], writes=[qh_b])
                op('dve', lambda e: e.tensor_tensor(ktl[:, :], kk[:, :], ek[:, :], ALU.mult),
                     reads=[kk_b, ek_b], writes=[ktl_b])

                def sc_grp(e):
                    ins = None
                    for c in range(16):
                        cs = slice(c * 32, (c + 1) * 32)
                        ins = e.matmul(pss[0:32, cs], lhsT=ktl[:, cs], rhs=qh[:, cs], start=True, stop=True)
                    return ins
                op('pe', sc_grp, reads=[ktl_b, qh_b], writes=[pss_b])
                op('dve', lambda e: e.tensor_tensor(PT[:, :], pss[0:32, :], cmask, ALU.mult),
                     reads=[pss_b, self.cst_b], writes=[PT_b])
                ptv = ptp[:, :].bitcast(BF16)
                for half in range(2):
                    def tr_grp(e, half=half):
                        ins = None
                        for c8 in range(8):
                            c = half * 8 + c8
                            ins = e.transpose(ptv[0:32, c8 * 128:(c8 + 1) * 128], ktl[:, c * 32:(c + 1) * 32],
                                              self.ident16[:, :])
                        return ins
                    op('pe', tr_grp, reads=[ktl_b, self.ident16_b], writes=[ptp_b])
                    op('act', lambda e, half=half: e.activation(
                        out=ktok[:, half * 8:(half + 1) * 8, :].rearrange("p c d -> p (c d)"),
                        in_=ptv[0:32, :], func=AF.Identity), reads=[ptp_b], writes=[ktok_b])
                td = tdall[si]
                for c in range(16):
                    pd, pd_b = pdrot.next()
                    dec = eb[:, c * 32 + 31:c * 32 + 32]
                    op('pe', lambda e, pd=pd, c=c: e.matmul(
                        pd[:, 0:DV], lhsT=ktok[:, c, :], rhs=vt[0:32, c, h * DV:(h + 1) * DV],
                        start=True, stop=True), reads=[ktok_b, vt_b], writes=[pd_b])
                    op('act', lambda e, pd=pd, c=c, dec=dec: e.activation(
                        out=td[:, c, :], in_=pd[:, 0:DV], func=AF.Identity, scale=dec),
                        reads=[pd_b, eb_b], writes=[tdall_b[si][c]])
                return dict(tt=tt, h=h, si=si, vt=vt, vt_b=vt_b, gt=gt, gt_b=gt_b, eb=eb, eb_b=eb_b, qh=qh, qh_b=qh_b,
                            PT=PT, PT_b=PT_b)

            def chain_gen(cx):
                h, si, eb, eb_b = cx['h'], cx['si'], cx['eb'], cx['eb_b']
                td = tdall[si]
                s16 = s16all[si]
                for c in range(16):
                    (sa, sa_b) = S32[h][c % 2]
                    (sn, sn_b) = S32[h][(c + 1) % 2]
                    dec = eb[:, c * 32 + 31:c * 32 + 32]
                    k.op('act', lambda e, c=c, sa=sa: e.activation(out=s16[:, c, :], in_=sa[:, :], func=AF.Identity),
                         reads=[sa_b], writes=[s16all_b[si][c]])
                    yield
                    k.op('dve', lambda e, c=c, sa=sa, sn=sn, dec=dec: e.scalar_tensor_tensor(
                        out=sn[:, :], in0=sa[:, :], scalar=dec, in1=td[:, c, :], op0=ALU.mult, op1=ALU.add),
                        reads=[sa_b, tdall_b[si][c], eb_b], writes=[sn_b])
                    yield

            def out(cx):
                tt, h, si = cx['tt'], cx['h'], cx['si']
                vt, vt_b, gt, gt_b = cx['vt'], cx['vt_b'], cx['gt'], cx['gt_b']
                qh, qh_b, PT, PT_b = cx['qh'], cx['qh_b'], cx['PT'], cx['PT_b']
                ts = slice(tt * TT, (tt + 1) * TT)
                s16 = s16all[si]
                pos = porot.next()
                for c in range(16):
                    cs = slice(c * 32, (c + 1) * 32)
                    for j in range(NV):
                        po, po_b = pos[j]

                        def o_grp(e, po=po, j=j, c=c, cs=cs):
                            e.matmul(po[:, cs], lhsT=vt[0:32, c, h * DV + j * 128:h * DV + (j + 1) * 128],
                                     rhs=PT[:, cs], start=True, stop=False)
                            return e.matmul(po[:, cs], lhsT=s16[:, c, j * 128:(j + 1) * 128], rhs=qh[:, cs],
                                            start=False, stop=True)
                        k.op('pe', o_grp, reads=[vt_b, PT_b, s16all_b[si][c], qh_b], writes=[po_b])
                yt, yt_b = yt_rot.next()
                (pm, pm_b) = pdrot.next()
                t1s = []
                for j in range(NV):
                    po, po_b = pos[j]
                    t1, t1_b = t1rot.next()
                    if isB:
                        k.op('dve', lambda e, t1=t1, po=po: e.tensor_tensor(t1[:, :], po[:, :], gt[:, 0, :], ALU.mult),
                             reads=[po_b, gt_b], writes=[t1_b])
                    else:
                        k.op('act', lambda e, t1=t1, po=po: e.activation(out=t1[:, :], in_=po[:, :], func=AF.Identity),
                             reads=[po_b], writes=[t1_b])
                    sq, sq_b = sqrot.next()
                    k.op('act', lambda e, sq=sq, t1=t1: e.activation(out=sq[:, :], in_=t1[:, :], func=AF.Square),
                         reads=[t1_b], writes=[sq_b])
                    k.op('pe', lambda e, sq=sq, j=j: e.matmul(pm[:, :], lhsT=self.ones32[:, :], rhs=sq[:, :],
                                                              start=(j == 0), stop=(j == NV - 1)),
                         reads=[sq_b, self.ones32_b], writes=[pm_b])
                    t1s.append((t1, t1_b))
                k.op('act', lambda e: e.activation(out=st[:, :], in_=pm[:, :], func=AF.Ln,
                                                   bias=self.epsc[:, 0:1], scale=1.0 / DV),
                     reads=[pm_b], writes=[st_b])
                k.op('act', lambda e: e.activation(out=rs[:, :], in_=st[:, :], func=AF.Exp, scale=-0.5), reads=[st_b], writes=[rs_b])
                for j in range(NV):
                    t1, t1_b = t1s[j]
                    if isB:
                        k.op('dve', lambda e, t1=t1, j=j: e.scalar_tensor_tensor(
                            out=yt[:, j, :], in0=t1[:, :], scalar=self.col('hog%d' % l), in1=rs[:, :],
                            op0=ALU.mult, op1=ALU.mult), reads=[t1_b, rs_b, self.sc_b], writes=[yt_b])
                    else:
                        k.op('dve', lambda e, t1=t1, j=j: e.scalar_tensor_tensor(
                            out=t1[:, :], in0=t1[:, :], scalar=self.col('glag%d' % l, j), in1=rs[:, :],
                            op0=ALU.mult, op1=ALU.mult), reads=[t1_b, rs_b, self.sc_b], writes=[t1_b])
                        k.op('dve', lambda e, t1=t1, j=j: e.tensor_tensor(yt[:, j, :], t1[:, :], gt[:, j, :], ALU.mult),
                             reads=[t1_b, gt_b], writes=[yt_b])
                k.store('sp', ydst[h * DV:(h + 1) * DV, ts].rearrange("(j p) t -> p j t", p=128), yt[:, :, :], yt_b)

            pending = None
            for tt in range(NT):
                vt, vt_b = vrot.next()
                k.load('sp', vt[:, :, :], vt_b, vsrc[tt * TT:(tt + 1) * TT, :].rearrange("(c p) n -> p c n", p=32))
                cg = cg_b = None
                if not isB:
                    cg, cg_b = cgrot.next()
                    k.load('sp', cg[:, :], cg_b, self.cgd[:, tt * TT:(tt + 1) * TT])
                for h in range(H):
                    if pending is None:
                        cx = pre(tt, h, vt, vt_b, cg, cg_b, lambda: None)
                    else:
                        g = chain_gen(pending)
                        cx = pre(tt, h, vt, vt_b, cg, cg_b, lambda g=g: next(g, None))
                        for _ in g:
                            pass
                        out(pending)
                    pending = cx
            for _ in chain_gen(pending):
                pass
            out(pending)

    def ph_merge(self, l):
        k = self.k
        go = (l * 3 + 1) * 16
        xsrc = self.x_cur.rearrange("(kc p) t -> p kc t", p=128)
        xdst = self.x.rearrange("(kc p) t -> p kc t", p=128)
        gv = self.gat.rearrange("(n kc p) t -> p n kc t", p=128, kc=KC)
        wbv = [self.w_br[l, n].rearrange("(kc p) d -> p kc d", p=128) for n in range(3)]
        wov = self.w_out[l].rearrange("(kc p) d -> p kc d", p=128)
        with k.phase():
            yrot = k.rot("my", 4, [128, 8, TT], BF16)
            wbrot = k.rot("mwb", 3, [128, 8, 256], BF16)
            worot = k.rot("mwo", 2, [128, KC, 256], BF16)
            grot = k.rot("mg", 4, [128, TT], BF16)
            maccs = [k.sbb("macc%d" % i_, [128, KC, TT], F32) for i_ in range(2)]
            m16s = [k.sbb("m16_%d" % i_, [128, KC, TT], BF16) for i_ in range(2)]
            tmprot = k.rot("mtmp", 2, [128, TT], F32)
            xorot = k.rot("mxo", 3, [128, TT], F32)
            xnrot = k.rot("mxn", 3, [128, TT], F32)
            psrot = Rot(self.pb[0:5])
            psrot2 = Rot(self.pb[5:8])
            for t2 in range(0, self.NT, 2):
                tts = list(range(t2, min(t2 + 2, self.NT)))
                tsl = [slice(tt * TT, (tt + 1) * TT) for tt in tts]
                for n in range(3):
                    subs = []
                    for si, tt in enumerate(tts):
                        yt, yt_b = yrot.next()
                        k.load('sp', yt[:, :, :], yt_b,
                               self.y[n].rearrange("(kc p) t -> p kc t", p=128)[:, :, tsl[si]])
                        subs.append((yt, yt_b))

                    def ev(j, c, m, pss, si, n=n):
                        (ps, ps_b), = pss
                        macc, macc_b = maccs[si]
                        m16, m16_b = m16s[si]
                        g, g_b = grot.next()
                        k.load('sp', g[:, :], g_b, gv[:, n, j, tsl[si]])
                        if n == 0:
                            k.op('dve', lambda e: e.tensor_tensor(macc[:, j, :], ps[:, :], g[:, :], ALU.mult),
                                 reads=[ps_b, g_b], writes=[macc_b])
                        else:
                            tp, tp_b = tmprot.next()
                            k.op('dve', lambda e: e.tensor_tensor(tp[:, :], ps[:, :], g[:, :], ALU.mult),
                                 reads=[ps_b, g_b], writes=[tp_b])
                            if n == 1:
                                k.op('dve', lambda e: e.tensor_tensor(macc[:, j, :], macc[:, j, :], tp[:, :], ALU.add),
                                     reads=[tp_b, macc_b], writes=[macc_b])
                            else:
                                k.op('dve', lambda e: e.tensor_tensor(m16[:, j, :], macc[:, j, :], tp[:, :], ALU.add),
                                     reads=[tp_b, macc_b], writes=[m16_b])
                    self._gemm([wbv[n]], 8, 0, D, subs, [wbrot], psrot, ev)

                def evo(j, c, m, pss, si):
                    (po, po_b), = pss
                    xo, xo_b = xorot.next()
                    k.load('sp', xo[:, :], xo_b, xsrc[:, j, tsl[si]])
                    xn, xn_b = xnrot.next()
                    k.op('dve', lambda e: e.scalar_tensor_tensor(
                        out=xn[:, :], in0=po[:, :], scalar=self.colsG[:, go + j:go + j + 1], in1=xo[:, :],
                        op0=ALU.mult, op1=ALU.add), reads=[po_b, xo_b, self.colsG_b], writes=[xn_b])
                    k.store('sp', xdst[:, j, tsl[si]], xn[:, :], xn_b)
                self._gemm([wov], KC, 0, D, [m16s[si] for si in range(len(tts))], [worot], psrot2, evo)
        self.x_cur = self.x


WEIGHT_KEYS = ['w_ada', 'ffn_w_gate', 'ffn_w_up', 'ffn_w_down', 'w_in', 'gla_w_gate_up', 'w_branch', 'w_out']


def make_in_maps(inputs, ncores, T):
    oht, cst = host_consts()
    maps = []
    shared = {kname: np.ascontiguousarray(np.asarray(inputs[kname], dtype=np.float32)) for kname in WEIGHT_KEYS}
    npi = {kk: np.asarray(v) for kk, v in inputs.items()}
    for b in range(ncores):
        m = dict(shared)
        m['xT'] = np.ascontiguousarray(npi['x'][b, :T].T.astype(np.float32))
        m['smallcols'] = host_smallcols(b, npi)
        m['cst'] = cst
        m['oht'] = oht
        maps.append(m)
    return maps


def kernel(**inputs):
    x = np.asarray(inputs['x'])
    B, T, _ = x.shape
    prog = Prog(T)
    nc = prog.build()
    maps = make_in_maps(inputs, B, T)
    res = run_bass_kernel_spmd(nc, maps, core_ids=list(range(B)))
    out = np.stack([np.ascontiguousarray(r['outT'].T) for r in res.results], axis=0)
    return out.astype(np.float32)
```
